# Optimizing a Trainium2 kernel written in Bass

```python
import jax, jax.numpy as jnp
from jax import lax
import numpy as np

D_MODEL = 1024
BATCH = 32
SEQ = 2048
DEPTH = 1
DEC_BATCH = 16
DEC_SEQ = 2048
PAST_LEN = 128

EPS = 1e-6
GLA_HEADS = 4
GLA_DK = 128
GLA_DV = 256
GLA_QK = GLA_HEADS * GLA_DK
GLA_VW = GLA_HEADS * GLA_DV
GLA_RANK = 16
GLA_TEMP = 16.0
GLA_CHUNK = 64
MLA_HEADS = 8
MLA_Q_RANK = 256
MLA_KV_RANK = 256
MLA_NOPE = 128
MLA_ROPE = 64
MLA_V = 128
MLA_VW = MLA_HEADS * MLA_V
ROPE_BASE = 10000.0
Q_BLOCK = 128
D_FF = ((8 * D_MODEL + 3 * 256 - 1) // (3 * 256)) * 256
IN_WIDTHS = (GLA_QK, GLA_QK, GLA_VW, GLA_VW, GLA_RANK, GLA_RANK,
             MLA_Q_RANK, MLA_KV_RANK, MLA_ROPE, D_MODEL, D_MODEL)
D_IN = 2 * GLA_QK + 2 * GLA_VW + 2 * GLA_RANK + MLA_Q_RANK + MLA_KV_RANK + MLA_ROPE + 2 * D_MODEL

kernel_name = "hybrid_gla_mla_gated_encoder"


def rms_norm(x, g):
    xf = x.astype(jnp.float32)
    y = xf * lax.rsqrt(jnp.mean(xf * xf, axis=-1, keepdims=True) + EPS)
    return (y * g.astype(jnp.float32)).astype(x.dtype)


def rope_tables(L):
    inv = ROPE_BASE ** (-jnp.arange(0, MLA_ROPE, 2, dtype=jnp.float32) / MLA_ROPE)
    ang = jnp.arange(L, dtype=jnp.float32)[:, None] * inv[None, :]
    return jnp.cos(ang), jnp.sin(ang)


def apply_rope(x, cos, sin):
    x1, x2 = jnp.split(x.astype(jnp.float32), 2, axis=-1)
    return jnp.concatenate([x1 * cos - x2 * sin, x1 * sin + x2 * cos], axis=-1).astype(x.dtype)


def gla_scan(q, k, v, log_a):
    B, L = q.shape[0], q.shape[1]
    nc = L // GLA_CHUNK

    def chunks(t):
        return t.astype(jnp.float32).reshape(B, nc, GLA_CHUNK, GLA_HEADS, t.shape[-1]).transpose(1, 0, 3, 2, 4)

    qc, kc, vc, ac = chunks(q), chunks(k), chunks(v), chunks(log_a)
    b = jnp.cumsum(ac, axis=3)
    b_end = b[:, :, :, -1:, :]
    q_dec = qc * jnp.exp(b)
    k_inv = kc * jnp.exp(-b)
    k_end = kc * jnp.exp(b_end - b)
    mask = jnp.tril(jnp.ones((GLA_CHUNK, GLA_CHUNK), dtype=bool))
    scores = jnp.einsum('nbhid,nbhjd->nbhij', q_dec, k_inv)
    o_intra = jnp.einsum('nbhij,nbhje->nbhie', jnp.where(mask, scores, 0.0), vc)

    def step(S, inp):
        q_n, k_n, v_n, dec_n = inp
        o_n = jnp.einsum('bhid,bhde->bhie', q_n, S)
        S = dec_n[..., None] * S + jnp.einsum('bhjd,bhje->bhde', k_n, v_n)
        return S, o_n

    S0 = jnp.zeros((B, GLA_HEADS, GLA_DK, GLA_DV), jnp.float32)
    _, o_inter = lax.scan(step, S0, (q_dec, k_end, vc, jnp.exp(b_end[:, :, :, 0, :])))
    o = o_intra + o_inter
    return o.transpose(1, 0, 3, 2, 4).reshape(B, L, GLA_HEADS, GLA_DV)


def gla_branch(q_raw, k_raw, v_raw, g_raw, af_raw, ab_raw, p):
    B, L = q_raw.shape[0], q_raw.shape[1]
    q = q_raw.reshape(B, L, GLA_HEADS, GLA_DK) * (GLA_DK ** -0.5)
    k = k_raw.reshape(B, L, GLA_HEADS, GLA_DK)
    v = v_raw.reshape(B, L, GLA_HEADS, GLA_DV)
    z_f = af_raw @ p['gla_wa_fwd'] + p['gla_ba_fwd']
    z_b = ab_raw @ p['gla_wa_bwd'] + p['gla_ba_bwd']
    log_a_f = (jax.nn.log_sigmoid(z_f.astype(jnp.float32)) / GLA_TEMP).reshape(B, L, GLA_HEADS, GLA_DK)
    log_a_b = (jax.nn.log_sigmoid(z_b.astype(jnp.float32)) / GLA_TEMP).reshape(B, L, GLA_HEADS, GLA_DK)
    o_f = gla_scan(q, k, v, log_a_f)
    o_b = gla_scan(q[:, ::-1], k[:, ::-1], v[:, ::-1], log_a_b[:, ::-1])[:, ::-1]
    o = rms_norm(o_f + o_b, p['gla_norm']).astype(q_raw.dtype)
    o = o.reshape(B, L, GLA_VW) * jax.nn.silu(g_raw)
    return o @ p['w_o_gla']


def mla_branch(cq_raw, ckv_raw, kr_raw, p):
    B, L = cq_raw.shape[0], cq_raw.shape[1]
    cos, sin = rope_tables(L)
    c_q = rms_norm(cq_raw, p['mla_norm_q'])
    q = (c_q @ p['w_uq']).reshape(B, L, MLA_HEADS, MLA_NOPE + MLA_ROPE)
    q_nope, q_pe = q[..., :MLA_NOPE], q[..., MLA_NOPE:]
    q_pe = apply_rope(q_pe, cos[:, None, :], sin[:, None, :])
    c_kv = rms_norm(ckv_raw, p['mla_norm_kv'])
    k_nope = (c_kv @ p['w_uk']).reshape(B, L, MLA_HEADS, MLA_NOPE)
    v = (c_kv @ p['w_uv']).reshape(B, L, MLA_HEADS, MLA_V)
    k_pe = apply_rope(kr_raw, cos, sin)
    scale = (MLA_NOPE + MLA_ROPE) ** -0.5
    nb = L // Q_BLOCK
    qn_b = q_nope.reshape(B, nb, Q_BLOCK, MLA_HEADS, MLA_NOPE).transpose(1, 0, 2, 3, 4)
    qp_b = q_pe.reshape(B, nb, Q_BLOCK, MLA_HEADS, MLA_ROPE).transpose(1, 0, 2, 3, 4)

    def attend(blk):
        qn, qp = blk
        s = jnp.einsum('bqhd,bkhd->bhqk', qn, k_nope) + jnp.einsum('bqhr,bkr->bhqk', qp, k_pe)
        pr = jax.nn.softmax(s.astype(jnp.float32) * scale, axis=-1).astype(v.dtype)
        return jnp.einsum('bhqk,bkhe->bqhe', pr, v)

    o = lax.map(attend, (qn_b, qp_b))
    o = o.transpose(1, 0, 2, 3, 4).reshape(B, L, MLA_VW)
    return o @ p['w_o_mla']


def encoder_layer(x, p):
    h = rms_norm(x, p['norm_mix_pre'])
    split_idx = np.cumsum(IN_WIDTHS)[:-1].tolist()
    (q_raw, k_raw, v_raw, g_raw, af_raw, ab_raw,
     cq_raw, ckv_raw, kr_raw, gate_a, gate_b) = jnp.split(h @ p['w_in'], split_idx, axis=-1)
    y_a = gla_branch(q_raw, k_raw, v_raw, g_raw, af_raw, ab_raw, p)
    y_b = mla_branch(cq_raw, ckv_raw, kr_raw, p)
    merged = jax.nn.sigmoid(gate_a) * y_a + jax.nn.sigmoid(gate_b) * y_b
    x = x + rms_norm(merged @ p['w_out'], p['norm_mix_post'])
    h = rms_norm(x, p['norm_ffn_pre'])
    f = (jax.nn.silu(h @ p['w_gate']) * (h @ p['w_up'])) @ p['w_down']
    return x + rms_norm(f, p['norm_ffn_post'])


def _w(key, shape, fan_in):
    return jax.random.normal(key, shape, jnp.float32) * (fan_in ** -0.5)


def _gain(key, shape):
    return 1.0 + 0.02 * jax.random.normal(key, shape, jnp.float32)


def _bias(key, shape):
    return 0.1 * jax.random.normal(key, shape, jnp.float32)


def setup_inputs(seed: int = 0) -> dict:
    key = jax.random.key(seed)
    ks = jax.random.split(key, 24)
    L = DEPTH
    return {
        'x_prompt': jax.random.normal(ks[0], (BATCH, SEQ, D_MODEL), jnp.float32),
        'x_sample': jax.random.normal(ks[1], (DEC_BATCH, DEC_SEQ, D_MODEL), jnp.float32),
        'norm_mix_pre': _gain(ks[2], (L, D_MODEL)),
        'w_in': _w(ks[3], (L, D_MODEL, D_IN), D_MODEL),
        'gla_wa_fwd': _w(ks[4], (L, GLA_RANK, GLA_QK), GLA_RANK),
        'gla_ba_fwd': _bias(ks[5], (L, GLA_QK)),
        'gla_wa_bwd': _w(ks[6], (L, GLA_RANK, GLA_QK), GLA_RANK),
        'gla_ba_bwd': _bias(ks[7], (L, GLA_QK)),
        'gla_norm': _gain(ks[8], (L, GLA_DV)),
        'w_o_gla': _w(ks[9], (L, GLA_VW, D_MODEL), GLA_VW),
        'mla_norm_q': _gain(ks[10], (L, MLA_Q_RANK)),
        'w_uq': _w(ks[11], (L, MLA_Q_RANK, MLA_HEADS * (MLA_NOPE + MLA_ROPE)), MLA_Q_RANK),
        'mla_norm_kv': _gain(ks[12], (L, MLA_KV_RANK)),
        'w_uk': _w(ks[13], (L, MLA_KV_RANK, MLA_HEADS * MLA_NOPE), MLA_KV_RANK),
        'w_uv': _w(ks[14], (L, MLA_KV_RANK, MLA_HEADS * MLA_V), MLA_KV_RANK),
        'w_o_mla': _w(ks[15], (L, MLA_VW, D_MODEL), MLA_VW),
        'w_out': _w(ks[16], (L, D_MODEL, D_MODEL), D_MODEL),
        'norm_mix_post': _gain(ks[17], (L, D_MODEL)),
        'norm_ffn_pre': _gain(ks[18], (L, D_MODEL)),
        'w_gate': _w(ks[19], (L, D_MODEL, D_FF), D_MODEL),
        'w_up': _w(ks[20], (L, D_MODEL, D_FF), D_MODEL),
        'w_down': _w(ks[21], (L, D_FF, D_MODEL), D_FF),
        'norm_ffn_post': _gain(ks[22], (L, D_MODEL)),
    }


def reference(x_prompt, x_sample, norm_mix_pre, w_in, gla_wa_fwd, gla_ba_fwd, gla_wa_bwd,
              gla_ba_bwd, gla_norm, w_o_gla, mla_norm_q, w_uq, mla_norm_kv, w_uk, w_uv,
              w_o_mla, w_out, norm_mix_post, norm_ffn_pre, w_gate, w_up, w_down, norm_ffn_post):
    params = dict(norm_mix_pre=norm_mix_pre, w_in=w_in, gla_wa_fwd=gla_wa_fwd,
                  gla_ba_fwd=gla_ba_fwd, gla_wa_bwd=gla_wa_bwd, gla_ba_bwd=gla_ba_bwd,
                  gla_norm=gla_norm, w_o_gla=w_o_gla, mla_norm_q=mla_norm_q, w_uq=w_uq,
                  mla_norm_kv=mla_norm_kv, w_uk=w_uk, w_uv=w_uv, w_o_mla=w_o_mla,
                  w_out=w_out, norm_mix_post=norm_mix_post, norm_ffn_pre=norm_ffn_pre,
                  w_gate=w_gate, w_up=w_up, w_down=w_down, norm_ffn_post=norm_ffn_post)

    def trunk(x):
        for layer in range(DEPTH):
            x = encoder_layer(x, {name: arr[layer] for name, arr in params.items()})
        return x

    y_prompt = trunk(x_prompt)
    y_sample = trunk(x_sample)
    return (y_prompt, y_sample)
```

```python
import contextlib
import numpy as np
import ml_dtypes
import concourse.bass as bass
import concourse.mybir as mybir
from concourse.bass_utils import run_bass_kernel_spmd

F32 = mybir.dt.float32
BF16 = mybir.dt.bfloat16
AF = mybir.ActivationFunctionType
ALU = mybir.AluOpType

L = 2048
NT = 16
D = 1024
KC = 8
DIN = 5728
DFF = 2816
NF = 22
EPS = 1e-6
OFF_Q, OFF_K, OFF_V, OFF_G, OFF_AF, OFF_CQ, OFF_CKV, OFF_KR, OFF_GA, OFF_GB, OFF_KRROT = (
    0, 512, 1024, 2048, 3072, 3104, 3360, 3616, 3680, 4704, 5728)
WIN_COLS = DIN + 64
WUQ_COLS = 1536 + 512
N_CORES = 8
SEQ_PER_CORE = 6
ARENA_WORDS = 53200


class T:
    __slots__ = ("writer", "readers")

    def __init__(self):
        self.writer = None
        self.readers = []


class Op:
    __slots__ = ("eng", "fn", "deps", "signal", "is_dma", "key", "val")

    def __init__(self, eng, fn, is_dma, key):
        self.eng, self.fn, self.is_dma, self.key = eng, fn, is_dma, key
        self.deps = []
        self.signal = False
        self.val = None


class Prog:
    ENGS = ("pe", "act", "dve", "pool", "sp")

    def __init__(self, nc):
        self.nc = nc
        self.ops = {e: [] for e in self.ENGS}
        self.last = {e: None for e in self.ENGS}
        self.dmas = []

    def op(self, eng, fn, reads=(), writes=(), dma=False, key=None):
        o = Op(eng, fn, dma, key)
        deps = {}
        for t in reads:
            if t.writer is not None:
                deps[id(t.writer)] = t.writer
        for t in writes:
            if t.writer is not None:
                deps[id(t.writer)] = t.writer
            for r in t.readers:
                deps[id(r)] = r
        for d in deps.values():
            if (not d.is_dma) and (not dma) and d.eng == "pe" and eng == "pe":
                continue
            d.signal = True
            o.deps.append(d)
        for t in reads:
            if not dma:
                t.readers = [r for r in t.readers if r.is_dma or r.eng != eng]
            t.readers.append(o)
        for t in writes:
            t.writer = o
            t.readers = []
        self.ops[eng].append(o)
        if dma:
            self.dmas.append(o)
        else:
            self.last[eng] = o
        return o

    def barrier(self):
        fr = [o for o in self.last.values() if o is not None] + self.dmas
        self.dmas = []
        for d in fr:
            d.signal = True
        for e in self.ENGS:
            o = Op(e, None, False, None)
            o.deps = list(fr)
            self.ops[e].append(o)

    def emit(self):
        nc = self.nc
        stack = contextlib.ExitStack()
        esem = {e: stack.enter_context(nc.semaphore("s_" + e)) for e in self.ENGS}
        dsem, dcount = {}, {}
        for e in self.ENGS:
            cnt = 0
            for o in self.ops[e]:
                if o.fn is None:
                    continue
                if o.is_dma:
                    k = o.key
                    if k not in dsem:
                        dsem[k] = stack.enter_context(nc.semaphore("d_" + str(k)))
                        dcount[k] = 0
                    dcount[k] += 16
                    o.val = (dsem[k], dcount[k])
                    o.signal = True
                elif o.signal:
                    cnt += 1
                    o.val = (esem[e], cnt)
        with stack:
            with nc.Block() as block:
                def mk(e):
                    def body(eng):
                        waited = {}
                        for o in self.ops[e]:
                            need = {}
                            for d in o.deps:
                                sem, v = d.val
                                if v > need.get(id(sem), (None, 0))[1]:
                                    need[id(sem)] = (sem, v)
                            for sem, v in need.values():
                                if waited.get(id(sem), 0) >= v:
                                    continue
                                waited[id(sem)] = v
                                eng.wait_ge(sem, v)
                            if o.fn is None:
                                continue
                            ins = o.fn(eng)
                            if o.signal:
                                ins.then_inc(o.val[0], 16 if o.is_dma else 1)
                    return body
                block.tensor(mk("pe"))
                block.scalar(mk("act"))
                block.vector(mk("dve"))
                block.gpsimd(mk("pool"))
                block.sync(mk("sp"))


class Arena:
    def __init__(self, ap, n):
        self.ap, self.n, self.off = ap, n, 0

    def f32(self, n):
        a = self.ap[:, self.off:self.off + n]
        self.off += n
        assert self.off <= self.n, ("arena overflow", self.off, self.n)
        return a

    def bf(self, n):
        w = (n + 1) // 2
        a = self.ap[:, self.off:self.off + w].bitcast(BF16)
        self.off += w
        assert self.off <= self.n, ("arena overflow", self.off, self.n)
        return a


def r3(ap, b):
    return ap.rearrange("p (a b) -> p a b", b=b)


class _Stop(Exception):
    pass


def build_program(nseq, dbg=False, stop=None):
    nc = bass.Bass("TRN2", target_bir_lowering=False)
    p = Prog(nc)
    ntok = nseq * L

    def stage(name):
        if stop == name:
            raise _Stop()

    def din(name, shape, dt=F32):
        return nc.dram_tensor(name, list(shape), dt, kind="ExternalInput").ap()

    def dscr(name, shape, dt=BF16):
        return nc.dram_tensor(name, list(shape), dt, kind="Internal").ap()

    x_d = din("x", [ntok, D])
    y_d = nc.dram_tensor("y", [ntok, D], F32, kind="ExternalOutput").ap()
    w_in_d = din("w_in", [D, DIN])
    w_og_d = din("w_o_gla", [D, D])
    w_uq_d = din("w_uq", [256, 1536])
    w_uk_d = din("w_uk", [256, 1024])
    w_uv_d = din("w_uv", [256, 1024])
    w_om_d = din("w_o_mla", [D, D])
    w_out_d = din("w_out", [D, D])
    w_g_d = din("w_gate", [D, DFF])
    w_u_d = din("w_up", [D, DFF])
    w_d_d = din("w_down", [DFF, D])
    g_pre_d = din("g_pre", [128, 8])
    g_q_d = din("g_q", [128, 2])
    g_kv_d = din("g_kv", [128, 2])
    g_gla_d = din("g_gla", [128, 2])
    g_ffn_d = din("g_ffn", [128, 8])
    g_p1_d = din("g_post1", [128, D])
    g_p2_d = din("g_post2", [128, D])
    wa_f_d = din("wa_f", [16, 512])
    wa_b_d = din("wa_b", [16, 512])
    ba_f_d = din("ba_f", [128, 4])
    ba_b_d = din("ba_b", [128, 4])
    ident_d = din("ident", [128, 128], BF16)
    cos_d = din("cosT", [64, L])
    sin_d = din("sinT", [64, L])
    mk_f_d = din("mask_f", [128, 128])
    mk_b_d = din("mask_b", [128, 128])

    s_win = dscr("s_win", [128, 8, WIN_COLS])
    s_wog = dscr("s_wog", [128, 8, D])
    s_wuq = dscr("s_wuq", [128, 2, WUQ_COLS])
    s_wuk = dscr("s_wuk", [128, 2, 1024])
    s_wuv = dscr("s_wuv", [128, 2, 1024])
    s_wom = dscr("s_wom", [128, 8, D])
    s_wout = dscr("s_wout", [128, 8, D])
    s_wg = dscr("s_wg", [128, 8, DFF])
    s_wu = dscr("s_wu", [128, 8, DFF])
    s_wd = dscr("s_wd", [128, NF, D])

    dbg_out = {}
    if dbg:
        for nm, shp in (("d_hT", [128, 8 * L]), ("d_ogT", [128, 8 * L]), ("d_omT", [128, 8 * L]),
                        ("d_C", [128, 8 * L])):
            dbg_out[nm] = nc.dram_tensor(nm, shp, BF16, kind="ExternalOutput").ap()

    arena_t = nc.alloc_sbuf_tensor("arena", [128, ARENA_WORDS], F32)
    psum_t = nc.alloc_psum_tensor("psum", [128, 4096], F32)
    AR = Arena(arena_t.ap(), ARENA_WORDS)
    PS = psum_t.ap()

    def bank(b, lo=0, hi=512):
        return PS[:, b * 512 + lo:b * 512 + hi]

    def bank_bf(b):
        return PS[:, b * 512:(b + 1) * 512].bitcast(BF16)

    LOCK = [T() for _ in range(8)]

    def pl(*aps):
        out = []
        for a in aps:
            if a is None or isinstance(a, (int, float)):
                continue
            if getattr(a, "name", None) == "psum":
                size = 4 if a.dtype == F32 else 2
                b = ((a.offset * size) % 16384) // 2048
                if LOCK[b] not in out:
                    out.append(LOCK[b])
        return out

    def DMA(out, in_, reads=(), writes=(), key=None, eng="sp"):
        return p.op(eng, lambda e: e.dma_start(out=out, in_=in_), reads, writes, dma=True, key=key)

    def ACT(out, in_, func, reads=(), writes=(), **kw):
        return p.op("act", lambda e: e.activation(out=out, in_=in_, func=func, **kw), reads, list(writes) + pl(out, in_))

    def TT(eng, out, in0, in1, op, reads=(), writes=()):
        return p.op(eng, lambda e: e.tensor_tensor(out=out, in0=in0, in1=in1, op=op), reads, list(writes) + pl(out, in0, in1))

    def TS(eng, out, in0, s1, s2, op0, op1=None, reads=(), writes=()):
        writes = list(writes) + pl(out, in0)
        if op1 is None:
            return p.op(eng, lambda e: e.tensor_scalar(out=out, in0=in0, scalar1=s1, scalar2=None, op0=op0), reads, writes)
        return p.op(eng, lambda e: e.tensor_scalar(out=out, in0=in0, scalar1=s1, scalar2=s2, op0=op0, op1=op1), reads, writes)

    def STT(out, in0, scalar, in1, op0, op1, reads=(), writes=()):
        return p.op("dve", lambda e: e.scalar_tensor_tensor(out=out, in0=in0, scalar=scalar, in1=in1, op0=op0, op1=op1), reads,
                    list(writes) + pl(out, in0, in1))

    def CP(eng, out, in_, reads=(), writes=()):
        writes = list(writes) + pl(out, in_)
        if eng == "act":
            return p.op("act", lambda e: e.activation(out=out, in_=in_, func=AF.Copy), reads, writes)
        return p.op(eng, lambda e: e.tensor_copy(out=out, in_=in_), reads, writes)

    def MM(out, pairs, reads=(), writes=()):
        pairs = list(pairs)
        writes = list(writes) + pl(out)

        def fn(e):
            n = len(pairs)
            for i, (l, r) in enumerate(pairs):
                ins = e.matmul(out, lhsT=l, rhs=r, start=(i == 0), stop=(i == n - 1))
            return ins
        return p.op("pe", fn, reads, writes)

    def TR(outs_ins, ident_ap, reads=(), writes=()):
        outs_ins = list(outs_ins)
        writes = list(writes) + pl(*[o for o, _ in outs_ins])

        def fn(e):
            for o, i in outs_ins:
                ins = e.transpose(out=o, in_=i, identity=ident_ap)
            return ins
        return p.op("pe", fn, reads, writes)

    ident = AR.bf(128)
    ones_bf = AR.bf(128)
    ones_f = AR.f32(128)
    mask_f = AR.f32(128)
    mask_b = AR.f32(128)
    g_pre = AR.f32(8)
    g_q = AR.f32(2)
    g_kv = AR.f32(2)
    g_gla = AR.f32(2)
    g_ffn = AR.f32(8)
    nba_f = AR.f32(4)
    nba_b = AR.f32(4)
    AR.off += 2
    wa_f_bf = AR.bf(512)
    wa_b_bf = AR.bf(512)
    g_p1 = AR.f32(D)
    g_p2 = AR.f32(D)
    NSTAT = 64
    stat_ap = AR.f32(NSTAT)
    stat_T = [T() for _ in range(NSTAT)]
    stat_i = [0]

    def stat():
        i = stat_i[0] % NSTAT
        stat_i[0] += 1
        return stat_ap[:, i:i + 1], stat_T[i]

    hT = r3(AR.bf(8 * L), L)
    BT = r3(AR.bf(8 * L), L)
    CT = r3(AR.bf(8 * L), L)
    T_hT = [T() for _ in range(NT)]
    T_og = [[T() for _ in range(NT)] for _ in range(4)]
    T_om = [[T() for _ in range(4)] for _ in range(8)]
    T_C = [[T() for _ in range(4)] for _ in range(8)]
    X0 = AR.off
    T_const = T()

    def _build_body():
        for dst, src in ((ident, ident_d), (mask_f, mk_f_d), (mask_b, mk_b_d), (g_pre, g_pre_d), (g_q, g_q_d),
                         (g_kv, g_kv_d), (g_gla, g_gla_d), (g_ffn, g_ffn_d), (g_p1, g_p1_d), (g_p2, g_p2_d),
                         (nba_f, ba_f_d), (nba_b, ba_b_d)):
            DMA(dst, src, key="const")
        wa_st_f = AR.f32(512)
        wa_st_b = AR.f32(512)
        p.op("pool", lambda e: e.memset(wa_st_f[0:32, :], 0.0))
        p.op("pool", lambda e: e.memset(wa_st_b[0:32, :], 0.0))
        p.op("pool", lambda e: e.memset(ones_f, 1.0))
        p.op("pool", lambda e: e.memset(ones_bf, 1.0))
        p.barrier()
        DMA(wa_st_f[0:16, :], wa_f_d, key="const")
        DMA(wa_st_b[16:32, :], wa_b_d, key="const")
        p.barrier()
        p.op("dve", lambda e: e.tensor_copy(out=wa_f_bf[0:32, :], in_=wa_st_f[0:32, :]))
        p.op("dve", lambda e: e.tensor_copy(out=wa_b_bf[0:32, :], in_=wa_st_b[0:32, :]))
        p.op("dve", lambda e: e.tensor_scalar(out=nba_f, in0=nba_f, scalar1=-1.0, scalar2=None, op0=ALU.mult))
        p.op("dve", lambda e: e.tensor_scalar(out=nba_b, in0=nba_b, scalar1=-1.0, scalar2=None, op0=ALU.mult))
        p.barrier()

        stage('const')
        AR.off = X0
        NSL = 6
        PW = 2048
        st_f = [AR.f32(PW) for _ in range(NSL)]
        st_b = [AR.bf(PW) for _ in range(NSL)]
        rot_b = [AR.bf(512) for _ in range(NSL)]
        T_sf = [T() for _ in range(NSL)]
        T_sb = [T() for _ in range(NSL)]
        T_rb = [T() for _ in range(NSL)]
        pc = [0]
        cast_engs = ("dve", "act")

        pieces = []

        def conv_piece(src, dst, gain, rot=None):
            pieces.append((src, dst, gain, rot))

        def piece_load(i):
            src, dst, gain, rot = pieces[i]
            sl = i % NSL
            n = src.shape[1]
            DMA(st_f[sl][:, 0:n], src, writes=[T_sf[sl]], key="pin%d" % sl)

        def flush_pieces(depth=4):
            n_ = len(pieces)
            for i in range(min(depth, n_)):
                piece_load(i)
            for i in range(n_):
                if i + depth < n_:
                    piece_load(i + depth)
                piece_rest(i)
            del pieces[:]

        def piece_rest(i):
            src, dst, gain, rot = pieces[i]
            sl = i % NSL
            n = src.shape[1]
            sf, sb = st_f[sl][:, 0:n], st_b[sl][:, 0:n]
            eng = cast_engs[i % 2]
            if gain is None:
                CP(eng, sb, sf, reads=[T_sf[sl]], writes=[T_sb[sl]])
            elif eng == "act":
                ACT(sb, sf, AF.Copy, reads=[T_sf[sl]], writes=[T_sb[sl]], scale=gain)
            else:
                TS(eng, sb, sf, gain, None, ALU.mult, reads=[T_sf[sl]], writes=[T_sb[sl]])
            if rot is not None:
                rot(sf, sl, gain)
            DMA(dst, sb, reads=[T_sb[sl]], key="pout%d" % sl)

        def conv_weight(w_d, s_d, nkc, ncols, gain_ap, rotf=None):
            for kc in range(nkc):
                g = None if gain_ap is None else gain_ap[:, kc:kc + 1]
                for c0 in range(0, ncols, PW):
                    c1 = min(ncols, c0 + PW)
                    rf = None
                    if rotf is not None:
                        rf = rotf(kc, c0, c1)
                    conv_piece(w_d[kc * 128:(kc + 1) * 128, c0:c1], s_d[:, kc, c0:c1], g, rf)

        def win_rot(kc, c0, c1):
            if not (c0 <= OFF_KR and OFF_KR + 64 <= c1):
                return None

            def f(sf, sl, gain):
                a = OFF_KR - c0
                rb = rot_b[sl]
                p.op("dve", lambda e: e.tensor_scalar(out=rb[:, 0:32], in0=sf[:, a + 32:a + 64], scalar1=gain, scalar2=-1.0,
                                                      op0=ALU.mult, op1=ALU.mult), [T_sf[sl]], [T_rb[sl]])
                p.op("dve", lambda e: e.tensor_scalar(out=rb[:, 32:64], in0=sf[:, a:a + 32], scalar1=gain, scalar2=None,
                                                      op0=ALU.mult), [T_sf[sl]], [T_rb[sl]])
                DMA(s_win[:, kc, OFF_KRROT:OFF_KRROT + 64], rb[:, 0:64], reads=[T_rb[sl]], key="prot%d" % sl)
            return f

        def wuq_rot(kc, c0, c1):
            def f(sf, sl, gain):
                rb = r3(rot_b[sl][:, 0:512], 64)
                s3 = r3(sf[:, 0:1536], 192)
                p.op("dve", lambda e: e.tensor_scalar(out=rb[:, :, 0:32], in0=s3[:, :, 160:192], scalar1=gain, scalar2=-1.0,
                                                      op0=ALU.mult, op1=ALU.mult), [T_sf[sl]], [T_rb[sl]])
                p.op("dve", lambda e: e.tensor_scalar(out=rb[:, :, 32:64], in0=s3[:, :, 128:160], scalar1=gain, scalar2=None,
                                                      op0=ALU.mult), [T_sf[sl]], [T_rb[sl]])
                DMA(s_wuq[:, kc, 1536:2048], rot_b[sl][:, 0:512], reads=[T_rb[sl]], key="prot%d" % sl)
            return f

        conv_weight(w_in_d, s_win, 8, DIN, g_pre, win_rot)
        conv_weight(w_uq_d, s_wuq, 2, 1536, g_q, wuq_rot)
        conv_weight(w_uk_d, s_wuk, 2, 1024, g_kv)
        conv_weight(w_uv_d, s_wuv, 2, 1024, g_kv)
        conv_weight(w_om_d, s_wom, 8, D, None)
        conv_weight(w_out_d, s_wout, 8, D, None)
        conv_weight(w_g_d, s_wg, 8, DFF, g_ffn)
        conv_weight(w_u_d, s_wu, 8, DFF, g_ffn)
        conv_weight(w_d_d, s_wd, NF, D, None)
        flush_pieces()
        p.barrier()
        for kc in range(8):
            conv_piece(w_og_d[kc * 128:(kc + 1) * 128, 0:D], s_wog[:, kc, 0:D], g_gla[:, (kc % 2):(kc % 2) + 1])
        flush_pieces()
        p.barrier()

        stage('prologue')
        def rstd_from_ss(ss, t_ss, n):
            r, t_r = stat()
            ACT(r, ss, AF.Ln, reads=[t_ss], writes=[t_r], scale=1.0 / n, bias=EPS)
            ACT(r, r, AF.Exp, reads=[t_r], writes=[t_r], scale=-0.5)
            return r, t_r

        def norm_transpose(src, t_src, dst3, t_dst, slot, xs_sl, T_xs, T_ptr, ncol=D, bt=None):
            ss, t_ss = stat()
            xs = xs_sl[slot]
            ACT(xs[:, 0:ncol], src, AF.Square, reads=[t_src], writes=[t_ss, T_xs[slot]], accum_out=ss)
            r, t_r = rstd_from_ss(ss, t_ss, ncol)
            ACT(xs[:, 0:ncol], src, AF.Copy, reads=[t_src, t_r], writes=[T_xs[slot]], scale=r)
            nk = ncol // 128
            pb = bank_bf(6 + slot if bt is None else bt)
            TR([(pb[:, k * 128:(k + 1) * 128], xs[:, k * 128:(k + 1) * 128]) for k in range(nk)], ident,
               reads=[T_xs[slot]], writes=[T_ptr[slot]])
            CP("dve", dst3, r3(pb[:, 0:ncol], 128), reads=[T_ptr[slot]], writes=[t_dst])

        def norm_part(src, t_src, xs, t_xs, ncol=D):
            ss, t_ss = stat()
            ACT(xs[:, 0:ncol], src, AF.Square, reads=[t_src], writes=[t_ss, t_xs], accum_out=ss)
            r, t_r = rstd_from_ss(ss, t_ss, ncol)
            ACT(xs[:, 0:ncol], src, AF.Copy, reads=[t_src, t_r], writes=[t_xs], scale=r)

        def tr_part(xs, t_xs, dst3, t_dst, bt, ncol=D):
            nk = ncol // 128
            pb = bank_bf(bt)
            TR([(pb[:, k * 128:(k + 1) * 128], xs[:, k * 128:(k + 1) * 128]) for k in range(nk)], ident, reads=[t_xs])
            CP("dve", dst3, r3(pb[:, 0:ncol], 128), writes=[t_dst])

        def wslice(s_d, c0, n):
            return s_d[:, :, c0:c0 + n]

        for s in range(nseq):
            tok0 = s * L
            AR.off = X0
            xsl = [AR.f32(D) for _ in range(3)]
            T_xsl = [T() for _ in range(3)]
            xs_sl = [AR.bf(D) for _ in range(2)]
            T_xs = [T() for _ in range(2)]
            T_ptr = [T() for _ in range(2)]
            for t in range(NT):
                sl = t % 3
                DMA(xsl[sl], x_d[tok0 + t * 128:tok0 + (t + 1) * 128, :], writes=[T_xsl[sl]], key="x%d" % sl)
                norm_transpose(xsl[sl], T_xsl[sl], hT[:, :, t * 128:(t + 1) * 128], T_hT[t], t % 2, xs_sl, T_xs, T_ptr)
            p.barrier()
            if dbg and s == 0:
                DMA(dbg_out["d_hT"], hT.rearrange("p a b -> p (a b)"), reads=T_hT, key="dbg")

            stage('p0')
            AR.off = X0
            afab = AR.bf(L)
            T_afab = [T() for _ in range(4)]
            w_afab = r3(AR.bf(8 * 32), 32)
            T_wafab = T()
            qT = AR.bf(L); kT = AR.bf(L)
            T_qT = [T() for _ in range(4)]; T_kT = [T() for _ in range(4)]
            cf32 = CT.rearrange("p a b -> p (a b)").bitcast(F32)
            T1s = [AR.f32(1024), cf32[:, 0:1024]]; T2s = [AR.f32(1024), cf32[:, 1024:2048]]
            T_T1s = [[T() for _ in range(8)] for _ in range(2)]; T_T2s = [[T() for _ in range(8)] for _ in range(2)]
            qdec = [AR.bf(L) for _ in range(2)]
            kinv = [AR.bf(L) for _ in range(2)]
            T_qdec = [[T() for _ in range(2)] for _ in range(2)]
            T_kinv = [[T() for _ in range(2)] for _ in range(2)]
            dec = [AR.f32(16) for _ in range(2)]
            T_dec = [[T() for _ in range(2)] for _ in range(2)]
            vh = r3(AR.bf(NT * 256), 256)
            T_vh = [T() for _ in range(NT)]
            opart = r3(AR.f32(NT * 256), 256)
            T_op = [T() for _ in range(NT)]
            Rst = [AR.f32(256) for _ in range(2)]
            T_R = [T() for _ in range(2)]
            Sbf = [AR.bf(256) for _ in range(2)]
            T_S = [T() for _ in range(2)]
            ATs = [[AR.bf(128) for _ in range(2)] for _ in range(2)]
            T_AT = [[T() for _ in range(2)] for _ in range(2)]
            ktok = [[AR.bf(128) for _ in range(2)] for _ in range(2)]
            T_ktok = [[T() for _ in range(2)] for _ in range(2)]
            ofin = [AR.f32(256) for _ in range(2)]
            T_ofin = [T() for _ in range(2)]
            gh = r3(AR.bf(NT * 256), 256)
            T_gh = [T() for _ in range(NT)]
            ogs = [AR.bf(256) for _ in range(2)]
            T_ogs = [T() for _ in range(2)]
            junkf = AR.f32(256)
            T_junk = T()
            zt = [AR.f32(512) for _ in range(2)]
            T_zt = [T() for _ in range(2)]
            wq = r3(AR.bf(8 * 128), 128); wk = r3(AR.bf(8 * 128), 128)
            wvg = r3(AR.bf(8 * 512), 512)
            wv = wvg[:, :, 0:256]; wgg = wvg[:, :, 256:512]
            T_wq, T_wk, T_wv, T_wgg = T(), T(), T(), T()
            T_pb = [T() for _ in range(8)]
            T_psc = [[T() for _ in range(2)] for _ in range(2)]
            T_po = [[T() for _ in range(2)] for _ in range(2)]
            T_pu = [[T() for _ in range(2)] for _ in range(2)]
            T_ptk = [[T() for _ in range(2)] for _ in range(2)]
            T_pg = [T() for _ in range(2)]
            T_pog = [T() for _ in range(2)]

            DMA(w_afab, wslice(s_win, OFF_AF, 32), writes=[T_wafab], key="wafab")
            for blk in range(4):
                pb = bank(blk % 2)
                MM(pb[0:32, :], [(w_afab[:, kc, :], hT[:, kc, blk * 512:(blk + 1) * 512]) for kc in range(KC)],
                   reads=[T_wafab] + T_hT[4 * blk:4 * blk + 4], writes=[T_pb[blk % 2]])
                CP("dve", afab[0:32, blk * 512:(blk + 1) * 512], pb[0:32, :], reads=[T_pb[blk % 2]], writes=[T_afab[blk]])

            stage('G_afab')
            def load_head_g(hh):
                DMA(wq, wslice(s_win, OFF_Q + hh * 128, 128), writes=[T_wq], key="wq")
                DMA(wk, wslice(s_win, OFF_K + hh * 128, 128), writes=[T_wk], key="wk")
                DMA(wv, wslice(s_win, OFF_V + hh * 256, 256), writes=[T_wv], key="wv")
                DMA(wgg, wslice(s_win, OFF_G + hh * 256, 256), writes=[T_wgg], key="wgg")
            load_head_g(0)
            for h in range(4):
                for blk in range(4):
                    hb = [hT[:, kc, blk * 512:(blk + 1) * 512] for kc in range(KC)]
                    rd = T_hT[4 * blk:4 * blk + 4]
                    MM(bank(0), [(wq[:, kc, :], hb[kc]) for kc in range(KC)], reads=[T_wq] + rd, writes=[T_pb[0]])
                    ACT(qT[:, blk * 512:(blk + 1) * 512], bank(0), AF.Copy, reads=[T_pb[0]], writes=[T_qT[blk]], scale=128.0 ** -0.5)
                    MM(bank(1), [(wk[:, kc, :], hb[kc]) for kc in range(KC)], reads=[T_wk] + rd, writes=[T_pb[1]])
                    CP("dve", kT[:, blk * 512:(blk + 1) * 512], bank(1), reads=[T_pb[1]], writes=[T_kT[blk]])
                def vg_tiles(t0_, t1_):
                    for t in range(t0_, t1_):
                        b = 2 + (t % 2)
                        MM(bank(b), [(hT[:, kc, t * 128:(t + 1) * 128], wvg[:, kc, :]) for kc in range(KC)],
                           reads=[T_wv, T_wgg, T_hT[t]], writes=[T_pb[b]])
                        CP("act", vh[:, t, :], bank(b, 0, 256), reads=[T_pb[b]], writes=[T_vh[t]])
                        ACT(gh[:, t, :], bank(b, 256, 512), AF.Silu, reads=[T_pb[b]], writes=[T_gh[t]])

                def stageA(sti):
                    d, hf = sti // 2, sti % 2
                    wa = wa_f_bf if d == 0 else wa_b_bf
                    nba = nba_f if d == 0 else nba_b
                    T1, T2, T_T1, T_T2 = T1s[sti % 2], T2s[sti % 2], T_T1s[sti % 2], T_T2s[sti % 2]
                    for b2 in range(2):
                        blk = hf * 2 + b2
                        pb = 4 + (blk % 2)
                        MM(bank(pb), [(wa[0:32, h * 128:(h + 1) * 128], afab[0:32, blk * 512:(blk + 1) * 512])],
                           reads=[T_afab[blk]], writes=[T_pb[pb]])
                        z = zt[blk % 2]
                        ACT(z, bank(pb), AF.Exp, reads=[T_pb[pb]], writes=[T_zt[blk % 2]], scale=-1.0, bias=nba[:, h:h + 1])
                        tw = T_T1[b2 * 4:(b2 + 1) * 4]
                        ACT(T1[:, b2 * 512:(b2 + 1) * 512], z, AF.Ln, reads=[T_zt[blk % 2]], writes=tw, bias=1.0)
                    vg_tiles(sti * 4, sti * 4 + 4)
                    for c in range(8):
                        cs = slice(c * 128, (c + 1) * 128)
                        T1c, T2c = T1[:, cs], T2[:, cs]
                        p.op("dve", (lambda T1c, T2c: lambda e: e.tensor_tensor_scan(out=T2c, data0=ones_f, data1=T1c, initial=0.0,
                                                                                  op0=ALU.mult, op1=ALU.add))(T1c, T2c),
                             reads=[T_T1[c]], writes=[T_T2[c]])
                        if d == 1:
                            STT(T1c, T1c, T2[:, c * 128 + 127:c * 128 + 128], T2c, ALU.add, ALU.subtract,
                                reads=[T_T1[c], T_T2[c]], writes=[T_T1[c]])

                def stageB(sti):
                    d, hf = sti // 2, sti % 2
                    T1, T2, T_T1, T_T2 = T1s[sti % 2], T2s[sti % 2], T_T1s[sti % 2], T_T2s[sti % 2]
                    src, T_src = (T2, T_T2) if d == 0 else (T1, T_T1)
                    oth, T_oth = (T1, T_T1) if d == 0 else (T2, T_T2)
                    hs = slice(hf * 1024, (hf + 1) * 1024)
                    if d == 0:
                        ACT(oth, src, AF.Exp, reads=T_src, writes=T_oth, scale=1.0 / 16)
                        ACT(src, src, AF.Exp, reads=T_src, writes=T_src, scale=-1.0 / 16)
                        E, T_E, EI, T_EI = src, T_src, oth, T_oth
                    else:
                        ACT(oth, src, AF.Exp, reads=T_src, writes=T_oth, scale=-1.0 / 16)
                        ACT(src, src, AF.Exp, reads=T_src, writes=T_src, scale=1.0 / 16)
                        E, T_E, EI, T_EI = oth, T_oth, src, T_src
                    TT("dve", qdec[d][:, hs], qT[:, hs], E, ALU.mult, reads=T_qT[2 * hf:2 * hf + 2] + T_E, writes=[T_qdec[d][hf]])
                    TT("pool", kinv[d][:, hs], kT[:, hs], EI, ALU.mult, reads=T_kT[2 * hf:2 * hf + 2] + T_EI, writes=[T_kinv[d][hf]])
                    e3 = r3(E, 128)
                    col = 127 if d == 0 else 0
                    CP("pool", r3(dec[d][:, hf * 8:(hf + 1) * 8], 1), e3[:, :, col:col + 1], reads=T_E, writes=[T_dec[d][hf]])

                stageA(0)
                for sti in range(4):
                    if sti + 1 < 4:
                        stageA(sti + 1)
                    stageB(sti)
                stage('G_dec')
                if h + 1 < 4:
                    load_head_g(h + 1)
                def cinfo(st):
                    out = []
                    for d in range(2):
                        c = (st, NT - 1 - st)[d]
                        csl = slice(c * 128, (c + 1) * 128)
                        out.append((c, c // 8, csl, qdec[d][:, csl], kinv[d][:, csl]))
                    return out

                def burst1(st):
                    sl = st % 2
                    info = cinfo(st)
                    pbf6 = bank_bf(6 + sl)
                    for d in range(2):
                        c, hf, csl, q_c, k_c = info[d]
                        MM(bank(0 + sl, d * 128, d * 128 + 128), [(k_c, q_c)], reads=[T_qdec[d][hf], T_kinv[d][hf]], writes=[T_psc[d][sl]])
                    for d in range(2):
                        c, hf, csl, q_c, k_c = info[d]
                        tks = pbf6[:, d * 128:(d + 1) * 128]
                        TR([(tks, k_c)], ident, reads=[T_kinv[d][hf]], writes=[T_ptk[d][sl]])
                    for d in range(2):
                        TT("dve", ATs[d][sl], bank(0 + sl, d * 128, d * 128 + 128), mask_f if d == 0 else mask_b, ALU.mult,
                           reads=[T_psc[d][sl]], writes=[T_AT[d][sl]])
                    for d in range(2):
                        tks = pbf6[:, d * 128:(d + 1) * 128]
                        CP("act", ktok[d][sl], tks, reads=[T_ptk[d][sl]], writes=[T_ktok[d][sl]])

                def burst2(st):
                    sl = st % 2
                    first = (st == 0)
                    info = cinfo(st)
                    for d in range(2):
                        c, hf, csl, q_c, k_c = info[d]
                        pu = bank(4 + sl, d * 256, d * 256 + 256)
                        MM(pu, [(ktok[d][sl], vh[:, c, :])], reads=[T_ktok[d][sl], T_vh[c]], writes=[T_pu[d][sl]])
                    for d in range(2):
                        c, hf, csl, q_c, k_c = info[d]
                        po = bank(2 + sl, d * 256, d * 256 + 256)
                        pairs = [(ATs[d][sl], vh[:, c, :])]
                        rds = [T_AT[d][sl], T_vh[c]]
                        if not first:
                            pairs.append((q_c, Sbf[d]))
                            rds += [T_qdec[d][hf], T_S[d]]
                        MM(po, pairs, reads=rds, writes=[T_po[d][sl]])
                    for d in range(2):
                        c, hf, csl, q_c, k_c = info[d]
                        pu = bank(4 + sl, d * 256, d * 256 + 256)
                        prev_c = c - 1 if d == 0 else c + 1
                        if first:
                            CP("dve", Rst[d], pu, reads=[T_pu[d][sl]], writes=[T_R[d]])
                        else:
                            STT(Rst[d], Rst[d], dec[d][:, prev_c:prev_c + 1], pu, ALU.mult, ALU.add,
                                reads=[T_R[d], T_pu[d][sl], T_dec[d][prev_c // 8]], writes=[T_R[d]])
                        if st < NT - 1:
                            ACT(Sbf[d], Rst[d], AF.Copy, reads=[T_R[d], T_dec[d][hf]], writes=[T_S[d]], scale=dec[d][:, c:c + 1])
                    for d in range(2):
                        c, hf, csl, q_c, k_c = info[d]
                        po = bank(2 + sl, d * 256, d * 256 + 256)
                        if st < NT // 2:
                            CP("act", opart[:, c, :], po, reads=[T_po[d][sl]], writes=[T_op[c]])
                def finalize(st):
                    sl = st % 2
                    info = cinfo(st)
                    for d in range(2):
                        c, hf, csl, q_c, k_c = info[d]
                        po = bank(2 + sl, d * 256, d * 256 + 256)
                        TT("dve", ofin[d], po, opart[:, c, :], ALU.add, reads=[T_po[d][sl], T_op[c]], writes=[T_ofin[d]])
                    if True:
                        rr = []
                        for fs in range(2):
                            ss, t_ss = stat()
                            ACT(junkf, ofin[fs], AF.Square, reads=[T_ofin[fs]], writes=[t_ss, T_junk], accum_out=ss)
                            rr.append(rstd_from_ss(ss, t_ss, 256))
                        pbf = bank_bf(6 + sl)
                        for fs in range(2):
                            c = info[fs][0]
                            STT(ogs[fs], ofin[fs], rr[fs][0], gh[:, c, :], ALU.mult, ALU.mult, reads=[T_ofin[fs], rr[fs][1], T_gh[c]], writes=[T_ogs[fs]])
                        for fs in range(2):
                            o0 = 512 + fs * 256
                            TR([(pbf[:, o0 + j * 128:o0 + (j + 1) * 128], ogs[fs][:, j * 128:(j + 1) * 128]) for j in range(2)], ident,
                               reads=[T_ogs[fs]], writes=[T_pog[fs]])
                        for fs in range(2):
                            c, hf, csl, q_c, k_c = info[fs]
                            o0 = 512 + fs * 256
                            CP("dve", BT[:, 2 * h:2 * h + 2, csl], r3(pbf[:, o0:o0 + 256], 128), reads=[T_pog[fs]], writes=[T_og[h][c]])

                burst1(0)
                for st in range(NT):
                    if st + 1 < NT:
                        burst1(st + 1)
                    burst2(st)
                    if st - 1 >= NT // 2:
                        finalize(st - 1)
                finalize(NT - 1)
            p.barrier()
            if dbg and s == 0:
                DMA(dbg_out["d_ogT"], BT.rearrange("p a b -> p (a b)"), key="dbg")
                p.barrier()

            def gate_merge(off_gate, s_wproj, first, keyp):
                AR.off = X0
                wga = [r3(AR.bf(8 * 128), 128) for _ in range(2)]
                wpr = [r3(AR.bf(8 * 128), 128) for _ in range(2)]
                T_wga = [T() for _ in range(2)]
                T_wpr = [T() for _ in range(2)]
                sg = [AR.f32(512) for _ in range(2)]
                T_sgm = [T() for _ in range(2)]
                tm = [AR.f32(512) for _ in range(2)]
                T_tm = [T() for _ in range(2)]
                T_p1 = [T() for _ in range(2)]
                T_p2 = [T() for _ in range(2)]

                def load(m):
                    DMA(wga[m % 2], wslice(s_win, off_gate + m * 128, 128), writes=[T_wga[m % 2]], key=keyp + "g%d" % (m % 2))
                    DMA(wpr[m % 2], wslice(s_wproj, m * 128, 128), writes=[T_wpr[m % 2]], key=keyp + "p%d" % (m % 2))
                load(0)
                it = 0
                for m in range(8):
                    if m + 1 < 8:
                        load(m + 1)
                    for blk in range(4):
                        i2 = it % 2
                        it += 1
                        bs = slice(blk * 512, (blk + 1) * 512)
                        MM(bank(i2 * 2), [(wga[m % 2][:, kc, :], hT[:, kc, bs]) for kc in range(KC)],
                           reads=[T_wga[m % 2]] + T_hT[4 * blk:4 * blk + 4], writes=[T_p1[i2]])
                        if first:
                            rdb = [T_og[hh][c] for hh in range(4) for c in range(4 * blk, 4 * blk + 4)]
                        else:
                            rdb = [T_om[hh][blk] for hh in range(8)]
                        MM(bank(i2 * 2 + 1), [(wpr[m % 2][:, kc, :], BT[:, kc, bs]) for kc in range(KC)],
                           reads=[T_wpr[m % 2]] + rdb, writes=[T_p2[i2]])
                        ACT(sg[i2], bank(i2 * 2), AF.Sigmoid, reads=[T_p1[i2]], writes=[T_sgm[i2]])
                        if first:
                            TT("dve", CT[:, m, bs], sg[i2], bank(i2 * 2 + 1), ALU.mult, reads=[T_sgm[i2], T_p2[i2]], writes=[T_C[m][blk]])
                        else:
                            TT("dve", tm[i2], sg[i2], bank(i2 * 2 + 1), ALU.mult, reads=[T_sgm[i2], T_p2[i2]], writes=[T_tm[i2]])
                            TT("pool", CT[:, m, bs], CT[:, m, bs], tm[i2], ALU.add, reads=[T_tm[i2], T_C[m][blk]], writes=[T_C[m][blk]])
                p.barrier()

            stage('G')
            gate_merge(OFF_GA, s_wog, True, "ga")
            stage('ga')

            AR.off = X0
            cqT = r3(AR.bf(2 * L), L); ckvT = r3(AR.bf(2 * L), L)
            T_cq = [T() for _ in range(NT)]; T_ckv = [T() for _ in range(NT)]
            kpeT = AR.bf(L)
            T_kpe = [T() for _ in range(4)]
            p.op("pool", lambda e: e.memset(kpeT[64:128, :], 0.0), writes=T_kpe)
            cosT = AR.f32(L); sinT = AR.f32(L)
            T_cos, T_sin = T(), T()
            NPT = 6
            PTs = [AR.bf(512) for _ in range(NPT)]
            T_PT = [T() for _ in range(NPT)]
            t1 = AR.f32(512); t2 = AR.f32(512)
            T_t1, T_t2 = T(), T()
            rz = AR.f32(512)
            T_rz = T()
            zacc = [[AR.f32(512) for _ in range(2)] for _ in range(2)]
            T_zacc = [[T() for _ in range(2)] for _ in range(2)]
            hw_ = []
            for _ in range(2):
                hw_.append(dict(qn=r3(AR.bf(2 * 128), 128), qp=r3(AR.bf(2 * 64), 64), qr=r3(AR.bf(2 * 64), 64),
                                kn=r3(AR.bf(2 * 128), 128), vn=r3(AR.bf(2 * 128), 128),
                                T=dict(qn=T(), qp=T(), qr=T(), kn=T(), vn=T())))

            def head_bufs():
                hb0 = dict(qn=AR.bf(L), qp=AR.bf(L), kn=AR.bf(L), v=r3(AR.bf(NT * 128), 128),
                           T_qn=[T() for _ in range(4)], T_qp=[T() for _ in range(4)], T_kn=[T() for _ in range(4)],
                           T_v=[T() for _ in range(4)])
                qp_ = hb0["qp"]
                p.op("pool", lambda e: e.memset(qp_[64:128, :], 0.0), writes=hb0["T_qp"])
                return hb0
            hb_ = [head_bufs(), None]
            mark_m = AR.off
            wlat = r3(AR.bf(8 * 512), 512)
            wkr = r3(AR.bf(8 * 64), 64); wkrr = r3(AR.bf(8 * 64), 64)
            T_wlat, T_wkr, T_wkrr = T(), T(), T()
            cl = [AR.bf(512) for _ in range(2)]
            T_cl = [T() for _ in range(2)]
            junk2 = AR.f32(256)
            T_junk2 = T()

            DMA(cosT[0:64, :], cos_d, writes=[T_cos], key="cos")
            DMA(sinT[0:64, :], sin_d, writes=[T_sin], key="sin")
            DMA(wlat, wslice(s_win, OFF_CQ, 512), writes=[T_wlat], key="wlat")
            DMA(wkr, wslice(s_win, OFF_KR, 64), writes=[T_wkr], key="wkr")
            DMA(wkrr, wslice(s_win, OFF_KRROT, 64), writes=[T_wkrr], key="wkrr")
            for t in range(NT):
                b = t % 2
                ts_ = slice(t * 128, (t + 1) * 128)
                MM(bank(b), [(hT[:, kc, ts_], wlat[:, kc, :]) for kc in range(KC)], reads=[T_wlat, T_hT[t]])
                rr = []
                for j in range(2):
                    ss, t_ss = stat()
                    ACT(junk2, bank(b, j * 256, j * 256 + 256), AF.Square, writes=[t_ss, T_junk2], accum_out=ss)
                    rr.append(rstd_from_ss(ss, t_ss, 256))
                for j in range(2):
                    ACT(cl[b][:, j * 256:(j + 1) * 256], bank(b, j * 256, j * 256 + 256), AF.Copy, reads=[rr[j][1]],
                        writes=[T_cl[b]], scale=rr[j][0])
                pbf = bank_bf(6 + b)
                TR([(pbf[:, k * 128:(k + 1) * 128], cl[b][:, k * 128:(k + 1) * 128]) for k in range(4)], ident, reads=[T_cl[b]])
                CP("dve", cqT[:, :, ts_], r3(pbf[:, 0:256], 128), writes=[T_cq[t]])
                CP("dve", ckvT[:, :, ts_], r3(pbf[:, 256:512], 128), writes=[T_ckv[t]])

            def rope_a(pa, bs):
                TT("dve", t1[0:64, :], pa, cosT[0:64, bs], ALU.mult, reads=[T_cos], writes=[T_t1])

            def rope_b(pb_, dst, t_dst, bs):
                TT("dve", t2[0:64, :], pb_, sinT[0:64, bs], ALU.mult, reads=[T_sin], writes=[T_t2])
                TT("pool", dst, t1[0:64, :], t2[0:64, :], ALU.add, reads=[T_t1, T_t2], writes=[t_dst])

            for blk in range(4):
                bs = slice(blk * 512, (blk + 1) * 512)
                rd = T_hT[4 * blk:4 * blk + 4]
                MM(bank(2)[0:64, :], [(wkr[:, kc, :], hT[:, kc, bs]) for kc in range(KC)], reads=[T_wkr] + rd)
                rope_a(bank(2)[0:64, :], bs)
                MM(bank(3)[0:64, :], [(wkrr[:, kc, :], hT[:, kc, bs]) for kc in range(KC)], reads=[T_wkrr] + rd)
                rope_b(bank(3)[0:64, :], kpeT[0:64, bs], T_kpe[blk], bs)
            p.barrier()
            AR.off = mark_m
            hb_[1] = head_bufs()

            SCALE = 192.0 ** -0.5
            pbk = [0]

            def prep_bank():
                b = 6 + (pbk[0] % 2)
                pbk[0] += 1
                return b

            def load_head_w(h):
                w = hw_[h % 2]
                DMA(w["qn"], wslice(s_wuq, h * 192, 128), writes=[w["T"]["qn"]], key="wqn%d" % (h % 2))
                DMA(w["qp"], wslice(s_wuq, h * 192 + 128, 64), writes=[w["T"]["qp"]], key="wqp%d" % (h % 2))
                DMA(w["qr"], wslice(s_wuq, 1536 + h * 64, 64), writes=[w["T"]["qr"]], key="wqr%d" % (h % 2))
                DMA(w["kn"], wslice(s_wuk, h * 128, 128), writes=[w["T"]["kn"]], key="wkn%d" % (h % 2))
                DMA(w["vn"], wslice(s_wuv, h * 128, 128), writes=[w["T"]["vn"]], key="wvn%d" % (h % 2))

            def prep_groups(h):
                w = hw_[h % 2]
                hb = hb_[h % 2]
                gs = []
                for blk in range(4):
                    bs = slice(blk * 512, (blk + 1) * 512)
                    rq = T_cq[4 * blk:4 * blk + 4]
                    rk = T_ckv[4 * blk:4 * blk + 4]

                    def g_qn(bs=bs, rq=rq, blk=blk):
                        b = prep_bank()
                        MM(bank(b), [(w["qn"][:, kc, :], cqT[:, kc, bs]) for kc in range(2)], reads=[w["T"]["qn"]] + rq)
                        CP("act", hb["qn"][:, bs], bank(b), writes=[hb["T_qn"][blk]])

                    def g_kn(bs=bs, rk=rk, blk=blk):
                        b = prep_bank()
                        MM(bank(b), [(w["kn"][:, kc, :], ckvT[:, kc, bs]) for kc in range(2)], reads=[w["T"]["kn"]] + rk)
                        CP("act", hb["kn"][:, bs], bank(b), writes=[hb["T_kn"][blk]])

                    def g_qa(bs=bs, rq=rq, blk=blk):
                        b = prep_bank()
                        MM(bank(b)[0:64, :], [(w["qp"][:, kc, :], cqT[:, kc, bs]) for kc in range(2)], reads=[w["T"]["qp"]] + rq)
                        rope_a(bank(b)[0:64, :], bs)

                    def g_qb(bs=bs, rq=rq, blk=blk):
                        b = prep_bank()
                        MM(bank(b)[0:64, :], [(w["qr"][:, kc, :], cqT[:, kc, bs]) for kc in range(2)], reads=[w["T"]["qr"]] + rq)
                        rope_b(bank(b)[0:64, :], hb["qp"][0:64, bs], hb["T_qp"][blk], bs)

                    def g_v(blk=blk):
                        b = prep_bank()
                        pv = bank(b)
                        for tt in range(4):
                            t = blk * 4 + tt
                            MM(pv[:, tt * 128:(tt + 1) * 128], [(ckvT[:, kc, t * 128:(t + 1) * 128], w["vn"][:, kc, :]) for kc in range(2)],
                               reads=[w["T"]["vn"], T_ckv[t]])
                        CP("dve", hb["v"][:, blk * 4:(blk + 1) * 4, :], r3(pv, 128), writes=[hb["T_v"][blk]])
                    gs += [g_qn, g_kn, g_qa, g_qb, g_v]
                return gs

            load_head_w(0)
            for g in prep_groups(0):
                g()
            for h in range(8):
                hb = hb_[h % 2]
                pend = []
                if h + 1 < 8:
                    load_head_w(h + 1)
                    pend = prep_groups(h + 1)
                iters = [(qb, kt) for qb in range(4) for kt in range(NT)]

                def emit_S(i, hb=hb):
                    qb, kt = iters[i]
                    sl = i % 3
                    ps = i % NPT
                    qs = slice(qb * 512, (qb + 1) * 512)
                    ks = slice(kt * 128, (kt + 1) * 128)
                    MM(bank(sl), [(hb["kn"][:, ks], hb["qn"][:, qs]), (kpeT[:, ks], hb["qp"][:, qs])],
                       reads=[hb["T_kn"][kt // 4], hb["T_qn"][qb], T_kpe[kt // 4], hb["T_qp"][qb]])
                    ACT(PTs[ps], bank(sl), AF.Exp, writes=[T_PT[ps]], scale=SCALE)
                emit_S(0)
                emit_S(1)
                for i, (qb, kt) in enumerate(iters):
                    ps = i % NPT
                    ob = qb % 2

                    def pv_fn(e, kt=kt, ps=ps, ob=ob, hb=hb):
                        return e.matmul(bank(3 + ob), lhsT=hb["v"][:, kt, :], rhs=PTs[ps], start=(kt == 0), stop=(kt == NT - 1))
                    p.op("pe", pv_fn, reads=[T_PT[ps], hb["T_v"][kt // 4]], writes=[LOCK[3 + ob]])
                    ze = kt % 2
                    zeng = "dve" if ze == 0 else "pool"
                    if kt < 2:
                        CP(zeng, zacc[ze][ob], PTs[ps], reads=[T_PT[ps]], writes=[T_zacc[ze][ob]])
                    else:
                        TT(zeng, zacc[ze][ob], zacc[ze][ob], PTs[ps], ALU.add, reads=[T_PT[ps], T_zacc[ze][ob]], writes=[T_zacc[ze][ob]])
                    if i + 2 < len(iters):
                        emit_S(i + 2)
                    if pend and i % 3 == 2:
                        pend.pop(0)()
                    if kt == NT - 1:
                        qs = slice(qb * 512, (qb + 1) * 512)
                        MM(bank(5), [(ones_f, zacc[0][ob]), (ones_f, zacc[1][ob])], reads=[T_zacc[0][ob], T_zacc[1][ob]])
                        p.op("dve", lambda e: e.reciprocal(out=rz, in_=bank(5)), writes=[T_rz, LOCK[5]])
                        TT("dve", BT[:, h, qs], bank(3 + ob), rz, ALU.mult, reads=[T_rz], writes=[T_om[h][qb]])
                while pend:
                    pend.pop(0)()
            p.barrier()
            if dbg and s == 0:
                DMA(dbg_out["d_omT"], BT.rearrange("p a b -> p (a b)"), key="dbg")
                p.barrier()

            stage('M')
            gate_merge(OFF_GB, s_wom, False, "gb")
            stage('gb')
            if dbg and s == 0:
                DMA(dbg_out["d_C"], CT.rearrange("p a b -> p (a b)"), key="dbg")
                p.barrier()

            AR.off = X0
            FB = 4
            wout = r3(AR.bf(8 * D), D)
            T_wout = T()
            hT_f32 = hT.rearrange("p a b -> p (a b)").bitcast(F32)
            x1 = [r3(hT_f32[:, i * FB * D:(i + 1) * FB * D], D) for i in range(2)]
            T_x1 = [[T() for _ in range(FB)] for _ in range(2)]
            xr = [AR.f32(D) for _ in range(2)]
            T_xr = [T() for _ in range(2)]
            tmpn = [AR.f32(D) for _ in range(2)]
            T_tmp = [T() for _ in range(2)]
            tmpy, T_tmpy = tmpn, T_tmp
            xs2 = [AR.bf(D) for _ in range(FB)]
            T_xs2 = [T() for _ in range(FB)]
            h2T = r3(AR.bf(8 * 512), 512)
            T_h2 = [T() for _ in range(FB)]
            aT = r3(BT.rearrange("p a b -> p (a b)")[:, 0:NF * 512], 512)
            T_a = [T() for _ in range(NF)]
            NWS = 3
            wgs = [r3(AR.bf(8 * 128), 128) for _ in range(NWS)]
            wus = [r3(AR.bf(8 * 128), 128) for _ in range(NWS)]
            T_wgs = [T() for _ in range(NWS)]
            T_wus = [T() for _ in range(NWS)]
            bflat = BT.rearrange("p a b -> p (a b)")
            wds = [AR.bf(D) for _ in range(NF - 5)] + [bflat[:, NF * 512 + j * D:NF * 512 + (j + 1) * D] for j in range(5)]
            T_wds = [T() for _ in range(NF)]
            sgt = [AR.f32(512) for _ in range(2)]
            T_sgt = [T() for _ in range(2)]
            junk3 = AR.bf(512)
            T_junk3 = T()
            T_dummy = [T(), T()]
            DMA(wout, s_wout, writes=[T_wout], key="wout")
            rbk = [0]

            def nbank():
                b = rbk[0] % 8
                rbk[0] += 1
                return b

            def post_norm_parts(b0, b1):
                ssa, t_a = stat()
                ssb, t_b = stat()
                ACT(junk3, bank(b0), AF.Square, writes=[t_a, T_junk3], accum_out=ssa)
                ACT(junk3, bank(b1), AF.Square, writes=[t_b, T_junk3], accum_out=ssb)
                sst, t_s = stat()
                TT("dve", sst, ssa, ssb, ALU.add, reads=[t_a, t_b], writes=[t_s])
                return rstd_from_ss(sst, t_s, D)

            def p3_tiles(blk):
                xb = blk % 2
                for tt in range(FB):
                    t = blk * FB + tt
                    ts_ = slice(t * 128, (t + 1) * 128)
                    i2 = tt % 2
                    b0, b1 = nbank(), nbank()
                    rd = [T_C[m][blk] for m in range(8)] + [T_wout]
                    MM(bank(b0), [(CT[:, kc, ts_], wout[:, kc, 0:512]) for kc in range(KC)], reads=rd)
                    MM(bank(b1), [(CT[:, kc, ts_], wout[:, kc, 512:1024]) for kc in range(KC)], reads=rd)
                    DMA(xr[i2], x_d[tok0 + t * 128:tok0 + (t + 1) * 128, :], writes=[T_xr[i2]], key="xr%d" % i2)
                    r, t_r = post_norm_parts(b0, b1)
                    STT(tmpn[i2][:, 0:512], bank(b0), r, g_p1[:, 0:512], ALU.mult, ALU.mult, reads=[t_r], writes=[T_tmp[i2]])
                    STT(tmpn[i2][:, 512:1024], bank(b1), r, g_p1[:, 512:1024], ALU.mult, ALU.mult, reads=[t_r], writes=[T_tmp[i2]])
                    TT("pool", x1[xb][:, tt, :], tmpn[i2], xr[i2], ALU.add, reads=[T_tmp[i2], T_xr[i2]], writes=[T_x1[xb][tt]])
                    norm_part(x1[xb][:, tt, :], T_x1[xb][tt], xs2[tt], T_xs2[tt])

            def p3_trans(blk):
                for tt in range(FB):
                    tr_part(xs2[tt], T_xs2[tt], h2T[:, :, tt * 128:(tt + 1) * 128], T_h2[tt], nbank())

            def load_gu(f):
                DMA(wgs[f % NWS], wslice(s_wg, f * 128, 128), writes=[T_wgs[f % NWS]], key="wg%d" % (f % NWS))
                DMA(wus[f % NWS], wslice(s_wu, f * 128, 128), writes=[T_wus[f % NWS]], key="wu%d" % (f % NWS))

            for f in range(NF):
                DMA(wds[f], s_wd[:, f, :], writes=[T_wds[f]], key="wd%d" % f)

            def ffn1(blk):
                load_gu(0)
                load_gu(1)
                for f in range(NF):
                    if f + 2 < NF:
                        load_gu(f + 2)
                    i2 = f % 2
                    bg, bu = nbank(), nbank()
                    MM(bank(bg), [(wgs[f % NWS][:, kc, :], h2T[:, kc, :]) for kc in range(KC)], reads=[T_wgs[f % NWS]] + T_h2)
                    MM(bank(bu), [(wus[f % NWS][:, kc, :], h2T[:, kc, :]) for kc in range(KC)], reads=[T_wus[f % NWS]] + T_h2)
                    ACT(sgt[i2], bank(bg), AF.Silu, writes=[T_sgt[i2]])
                    TT("dve", aT[:, f, :], sgt[i2], bank(bu), ALU.mult, reads=[T_sgt[i2]], writes=[T_a[f]])

            def ffn2(blk):
                xb = blk % 2
                for pr in range(FB // 2):
                    bks = [nbank() for _ in range(4)]
                    for f in range(NF):
                        w = wds[f]

                        def dfn(e, f=f, w=w, pr=pr, bks=bks):
                            for j in range(2):
                                tt = pr * 2 + j
                                a = aT[:, f, tt * 128:(tt + 1) * 128]
                                e.matmul(bank(bks[2 * j]), lhsT=a, rhs=w[:, 0:512], start=(f == 0), stop=(f == NF - 1))
                                ins = e.matmul(bank(bks[2 * j + 1]), lhsT=a, rhs=w[:, 512:1024], start=(f == 0), stop=(f == NF - 1))
                            return ins
                        p.op("pe", dfn, reads=[T_a[f], T_wds[f]], writes=[LOCK[b] for b in bks])
                    for j in range(2):
                        tt = pr * 2 + j
                        t = blk * FB + tt
                        b0, b1 = bks[2 * j], bks[2 * j + 1]
                        r, t_r = post_norm_parts(b0, b1)
                        STT(tmpy[j][:, 0:512], bank(b0), r, g_p2[:, 0:512], ALU.mult, ALU.mult, reads=[t_r], writes=[T_tmpy[j]])
                        STT(tmpy[j][:, 512:1024], bank(b1), r, g_p2[:, 512:1024], ALU.mult, ALU.mult, reads=[t_r], writes=[T_tmpy[j]])
                        TT("pool", tmpy[j], tmpy[j], x1[xb][:, tt, :], ALU.add, reads=[T_tmpy[j], T_x1[xb][tt]], writes=[T_tmpy[j]])
                        DMA(y_d[tok0 + t * 128:tok0 + (t + 1) * 128, :], tmpy[j], reads=[T_tmpy[j]], key="y%d" % j, eng="pool")

            p3_tiles(0)
            p3_trans(0)
            for blk in range(4):
                ffn1(blk)
                if blk + 1 < 4:
                    p3_tiles(blk + 1)
                ffn2(blk)
                if blk + 1 < 4:
                    p3_trans(blk + 1)
            p.barrier()


    try:
        _build_body()
    except _Stop:
        p.barrier()
    p.emit()
    return nc, list(dbg_out.keys())


def _host_consts():
    c = {}
    c["ident"] = np.eye(128, dtype=np.float32).astype(ml_dtypes.bfloat16)
    inv = (10000.0 ** (-np.arange(0, 64, 2, dtype=np.float32) / np.float32(64))).astype(np.float32)
    ang = (np.arange(L, dtype=np.float32)[:, None] * inv[None, :]).astype(np.float32)
    cos = np.cos(ang).astype(np.float32).T
    sin = np.sin(ang).astype(np.float32).T
    c["cosT"] = np.ascontiguousarray(np.concatenate([cos, cos], axis=0))
    c["sinT"] = np.ascontiguousarray(np.concatenate([sin, sin], axis=0))
    j = np.arange(128)[:, None]
    i = np.arange(128)[None, :]
    c["mask_f"] = (i >= j).astype(np.float32)
    c["mask_b"] = (i <= j).astype(np.float32)
    return c


def _pk(v, n):
    return np.ascontiguousarray(np.asarray(v, np.float32).reshape(n, 128).T)


def make_in_maps(inputs, x_shards):
    f = lambda k: np.ascontiguousarray(np.asarray(inputs[k], np.float32)[0])
    common = dict(
        w_in=f("w_in"), w_o_gla=f("w_o_gla"), w_uq=f("w_uq"), w_uk=f("w_uk"), w_uv=f("w_uv"),
        w_o_mla=f("w_o_mla"), w_out=f("w_out"), w_gate=f("w_gate"), w_up=f("w_up"), w_down=f("w_down"),
        g_pre=_pk(f("norm_mix_pre"), 8), g_q=_pk(f("mla_norm_q"), 2), g_kv=_pk(f("mla_norm_kv"), 2),
        g_gla=_pk(f("gla_norm"), 2), g_ffn=_pk(f("norm_ffn_pre"), 8),
        g_post1=np.ascontiguousarray(np.broadcast_to(f("norm_mix_post")[None, :], (128, D))),
        g_post2=np.ascontiguousarray(np.broadcast_to(f("norm_ffn_post")[None, :], (128, D))),
        wa_f=f("gla_wa_fwd"), wa_b=f("gla_wa_bwd"),
        ba_f=_pk(f("gla_ba_fwd"), 4), ba_b=_pk(f("gla_ba_bwd"), 4),
    )
    common.update(_host_consts())
    return [dict(common, x=xs) for xs in x_shards]


def kernel(**inputs):
    xp = np.asarray(inputs["x_prompt"], np.float32)
    xs = np.asarray(inputs["x_sample"], np.float32)
    nb_p, nb_s = xp.shape[0], xs.shape[0]
    allx = np.concatenate([xp.reshape(nb_p, L, D), xs.reshape(nb_s, L, D)], axis=0)
    nseq_total = allx.shape[0]
    per = nseq_total // N_CORES
    shards = [np.ascontiguousarray(allx[c * per:(c + 1) * per].reshape(per * L, D)) for c in range(N_CORES)]
    nc, _ = build_program(per)
    in_maps = make_in_maps(inputs, shards)
    res = run_bass_kernel_spmd(nc, in_maps, core_ids=list(range(N_CORES)))
    y = np.concatenate([np.asarray(r["y"], np.float32).reshape(per, L, D) for r in res.results], axis=0)
    return (np.ascontiguousarray(y[:nb_p]), np.ascontiguousarray(y[nb_p:]))
```

```python
import contextlib
import numpy as np
import ml_dtypes
import concourse.bass as bass
import concourse.mybir as mybir
from concourse.bass_utils import run_bass_kernel_spmd

F32 = mybir.dt.float32
BF16 = mybir.dt.bfloat16
AF = mybir.ActivationFunctionType
ALU = mybir.AluOpType

L = 2048
NT = 16
D = 1024
KC = 8
DIN = 5728
DFF = 2816
NF = 22
EPS = 1e-6
OFF_Q, OFF_K, OFF_V, OFF_G, OFF_AF, OFF_CQ, OFF_CKV, OFF_KR, OFF_GA, OFF_GB, OFF_KRROT = (
    0, 512, 1024, 2048, 3072, 3104, 3360, 3616, 3680, 4704, 5728)
WIN_COLS = DIN + 64
WUQ_COLS = 1536 + 512
N_CORES = 8
SEQ_PER_CORE = 6
ARENA_WORDS = 53200


class T:
    __slots__ = ("writer", "readers")

    def __init__(self):
        self.writer = None
        self.readers = []


class Op:
    __slots__ = ("eng", "fn", "deps", "signal", "is_dma", "key", "val")

    def __init__(self, eng, fn, is_dma, key):
        self.eng, self.fn, self.is_dma, self.key = eng, fn, is_dma, key
        self.deps = []
        self.signal = False
        self.val = None


class Prog:
    ENGS = ("pe", "act", "dve", "pool", "sp")

    def __init__(self, nc):
        self.nc = nc
        self.ops = {e: [] for e in self.ENGS}
        self.last = {e: None for e in self.ENGS}
        self.dmas = []

    def op(self, eng, fn, reads=(), writes=(), dma=False, key=None):
        o = Op(eng, fn, dma, key)
        deps = {}
        for t in reads:
            if t.writer is not None:
                deps[id(t.writer)] = t.writer
        for t in writes:
            if t.writer is not None:
                deps[id(t.writer)] = t.writer
            for r in t.readers:
                deps[id(r)] = r
        for d in deps.values():
            if (not d.is_dma) and (not dma) and d.eng == "pe" and eng == "pe":
                continue
            d.signal = True
            o.deps.append(d)
        for t in reads:
            if not dma:
                t.readers = [r for r in t.readers if r.is_dma or r.eng != eng]
            t.readers.append(o)
        for t in writes:
            t.writer = o
            t.readers = []
        self.ops[eng].append(o)
        if dma:
            self.dmas.append(o)
        else:
            self.last[eng] = o
        return o

    def barrier(self):
        fr = [o for o in self.last.values() if o is not None] + self.dmas
        self.dmas = []
        for d in fr:
            d.signal = True
        for e in self.ENGS:
            o = Op(e, None, False, None)
            o.deps = list(fr)
            self.ops[e].append(o)

    def emit(self):
        nc = self.nc
        stack = contextlib.ExitStack()
        esem = {e: stack.enter_context(nc.semaphore("s_" + e)) for e in self.ENGS}
        dsem, dcount = {}, {}
        for e in self.ENGS:
            cnt = 0
            for o in self.ops[e]:
                if o.fn is None:
                    continue
                if o.is_dma:
                    k = o.key
                    if k not in dsem:
                        dsem[k] = stack.enter_context(nc.semaphore("d_" + str(k)))
                        dcount[k] = 0
                    dcount[k] += 16
                    o.val = (dsem[k], dcount[k])
                    o.signal = True
                elif o.signal:
                    cnt += 1
                    o.val = (esem[e], cnt)
        with stack:
            with nc.Block() as block:
                def mk(e):
                    def body(eng):
                        waited = {}
                        for o in self.ops[e]:
                            need = {}
                            for d in o.deps:
                                sem, v = d.val
                                if v > need.get(id(sem), (None, 0))[1]:
                                    need[id(sem)] = (sem, v)
                            for sem, v in need.values():
                                if waited.get(id(sem), 0) >= v:
                                    continue
                                waited[id(sem)] = v
                                eng.wait_ge(sem, v)
                            if o.fn is None:
                                continue
                            ins = o.fn(eng)
                            if o.signal:
                                ins.then_inc(o.val[0], 16 if o.is_dma else 1)
                    return body
                block.tensor(mk("pe"))
                block.scalar(mk("act"))
                block.vector(mk("dve"))
                block.gpsimd(mk("pool"))
                block.sync(mk("sp"))


class Arena:
    def __init__(self, ap, n):
        self.ap, self.n, self.off = ap, n, 0

    def f32(self, n):
        a = self.ap[:, self.off:self.off + n]
        self.off += n
        assert self.off <= self.n, ("arena overflow", self.off, self.n)
        return a

    def bf(self, n):
        w = (n + 1) // 2
        a = self.ap[:, self.off:self.off + w].bitcast(BF16)
        self.off += w
        assert self.off <= self.n, ("arena overflow", self.off, self.n)
        return a


def r3(ap, b):
    return ap.rearrange("p (a b) -> p a b", b=b)


class _Stop(Exception):
    pass


def build_program(nseq, dbg=False, stop=None):
    nc = bass.Bass("TRN2", target_bir_lowering=False)
    p = Prog(nc)
    ntok = nseq * L

    def stage(name):
        if stop == name:
            raise _Stop()

    def din(name, shape, dt=F32):
        return nc.dram_tensor(name, list(shape), dt, kind="ExternalInput").ap()

    def dscr(name, shape, dt=BF16):
        return nc.dram_tensor(name, list(shape), dt, kind="Internal").ap()

    x_d = din("x", [ntok, D])
    y_d = nc.dram_tensor("y", [ntok, D], F32, kind="ExternalOutput").ap()
    w_in_d = din("w_in", [D, DIN])
    w_og_d = din("w_o_gla", [D, D])
    w_uq_d = din("w_uq", [256, 1536])
    w_uk_d = din("w_uk", [256, 1024])
    w_uv_d = din("w_uv", [256, 1024])
    w_om_d = din("w_o_mla", [D, D])
    w_out_d = din("w_out", [D, D])
    w_g_d = din("w_gate", [D, DFF])
    w_u_d = din("w_up", [D, DFF])
    w_d_d = din("w_down", [DFF, D])
    g_pre_d = din("g_pre", [128, 8])
    g_q_d = din("g_q", [128, 2])
    g_kv_d = din("g_kv", [128, 2])
    g_gla_d = din("g_gla", [128, 2])
    g_ffn_d = din("g_ffn", [128, 8])
    g_p1_d = din("g_post1", [128, D])
    g_p2_d = din("g_post2", [128, D])
    wa_f_d = din("wa_f", [16, 512])
    wa_b_d = din("wa_b", [16, 512])
    ba_f_d = din("ba_f", [128, 4])
    ba_b_d = din("ba_b", [128, 4])
    ident_d = din("ident", [128, 128], BF16)
    cos_d = din("cosT", [64, L])
    sin_d = din("sinT", [64, L])
    mk_f_d = din("mask_f", [128, 128])
    mk_b_d = din("mask_b", [128, 128])

    s_win = dscr("s_win", [128, 8, WIN_COLS])
    s_wog = dscr("s_wog", [128, 8, D])
    s_wuq = dscr("s_wuq", [128, 2, WUQ_COLS])
    s_wuk = dscr("s_wuk", [128, 2, 1024])
    s_wuv = dscr("s_wuv", [128, 2, 1024])
    s_wom = dscr("s_wom", [128, 8, D])
    s_wout = dscr("s_wout", [128, 8, D])
    s_wg = dscr("s_wg", [128, 8, DFF])
    s_wu = dscr("s_wu", [128, 8, DFF])
    s_wd = dscr("s_wd", [128, NF, D])

    dbg_out = {}
    if dbg:
        for nm, shp in (("d_hT", [128, 8 * L]), ("d_ogT", [128, 8 * L]), ("d_omT", [128, 8 * L]),
                        ("d_C", [128, 8 * L])):
            dbg_out[nm] = nc.dram_tensor(nm, shp, BF16, kind="ExternalOutput").ap()

    arena_t = nc.alloc_sbuf_tensor("arena", [128, ARENA_WORDS], F32)
    psum_t = nc.alloc_psum_tensor("psum", [128, 4096], F32)
    AR = Arena(arena_t.ap(), ARENA_WORDS)
    PS = psum_t.ap()

    def bank(b, lo=0, hi=512):
        return PS[:, b * 512 + lo:b * 512 + hi]

    def bank_bf(b):
        return PS[:, b * 512:(b + 1) * 512].bitcast(BF16)

    LOCK = [T() for _ in range(8)]

    def pl(*aps):
        out = []
        for a in aps:
            if a is None or isinstance(a, (int, float)):
                continue
            if getattr(a, "name", None) == "psum":
                size = 4 if a.dtype == F32 else 2
                b = ((a.offset * size) % 16384) // 2048
                if LOCK[b] not in out:
                    out.append(LOCK[b])
        return out

    def DMA(out, in_, reads=(), writes=(), key=None, eng="sp"):
        return p.op(eng, lambda e: e.dma_start(out=out, in_=in_), reads, writes, dma=True, key=key)

    def ACT(out, in_, func, reads=(), writes=(), **kw):
        return p.op("act", lambda e: e.activation(out=out, in_=in_, func=func, **kw), reads, list(writes) + pl(out, in_))

    def TT(eng, out, in0, in1, op, reads=(), writes=()):
        return p.op(eng, lambda e: e.tensor_tensor(out=out, in0=in0, in1=in1, op=op), reads, list(writes) + pl(out, in0, in1))

    def TS(eng, out, in0, s1, s2, op0, op1=None, reads=(), writes=()):
        writes = list(writes) + pl(out, in0)
        if op1 is None:
            return p.op(eng, lambda e: e.tensor_scalar(out=out, in0=in0, scalar1=s1, scalar2=None, op0=op0), reads, writes)
        return p.op(eng, lambda e: e.tensor_scalar(out=out, in0=in0, scalar1=s1, scalar2=s2, op0=op0, op1=op1), reads, writes)

    def STT(out, in0, scalar, in1, op0, op1, reads=(), writes=()):
        return p.op("dve", lambda e: e.scalar_tensor_tensor(out=out, in0=in0, scalar=scalar, in1=in1, op0=op0, op1=op1), reads,
                    list(writes) + pl(out, in0, in1))

    def CP(eng, out, in_, reads=(), writes=()):
        writes = list(writes) + pl(out, in_)
        if eng == "act":
            return p.op("act", lambda e: e.activation(out=out, in_=in_, func=AF.Copy), reads, writes)
        return p.op(eng, lambda e: e.tensor_copy(out=out, in_=in_), reads, writes)

    def MM(out, pairs, reads=(), writes=()):
        pairs = list(pairs)
        writes = list(writes) + pl(out)

        def fn(e):
            n = len(pairs)
            for i, (l, r) in enumerate(pairs):
                ins = e.matmul(out, lhsT=l, rhs=r, start=(i == 0), stop=(i == n - 1))
            return ins
        return p.op("pe", fn, reads, writes)

    def TR(outs_ins, ident_ap, reads=(), writes=()):
        outs_ins = list(outs_ins)
        writes = list(writes) + pl(*[o for o, _ in outs_ins])

        def fn(e):
            for o, i in outs_ins:
                ins = e.transpose(out=o, in_=i, identity=ident_ap)
            return ins
        return p.op("pe", fn, reads, writes)

    ident = AR.bf(128)
    ones_bf = AR.bf(128)
    ones_f = AR.f32(128)
    mask_f = AR.f32(128)
    mask_b = AR.f32(128)
    g_pre = AR.f32(8)
    g_q = AR.f32(2)
    g_kv = AR.f32(2)
    g_gla = AR.f32(2)
    g_ffn = AR.f32(8)
    nba_f = AR.f32(4)
    nba_b = AR.f32(4)
    AR.off += 2
    wa_f_bf = AR.bf(512)
    wa_b_bf = AR.bf(512)
    g_p1 = AR.f32(D)
    g_p2 = AR.f32(D)
    NSTAT = 64
    stat_ap = AR.f32(NSTAT)
    stat_T = [T() for _ in range(NSTAT)]
    stat_i = [0]

    def stat():
        i = stat_i[0] % NSTAT
        stat_i[0] += 1
        return stat_ap[:, i:i + 1], stat_T[i]

    hT = r3(AR.bf(8 * L), L)
    BT = r3(AR.bf(8 * L), L)
    CT = r3(AR.bf(8 * L), L)
    T_hT = [T() for _ in range(NT)]
    T_og = [[T() for _ in range(NT)] for _ in range(4)]
    T_om = [[T() for _ in range(4)] for _ in range(8)]
    T_C = [[T() for _ in range(4)] for _ in range(8)]
    X0 = AR.off
    T_const = T()

    def _build_body():
        for dst, src in ((ident, ident_d), (mask_f, mk_f_d), (mask_b, mk_b_d), (g_pre, g_pre_d), (g_q, g_q_d),
                         (g_kv, g_kv_d), (g_gla, g_gla_d), (g_ffn, g_ffn_d), (g_p1, g_p1_d), (g_p2, g_p2_d),
                         (nba_f, ba_f_d), (nba_b, ba_b_d)):
            DMA(dst, src, key="const")
        wa_st_f = AR.f32(512)
        wa_st_b = AR.f32(512)
        p.op("pool", lambda e: e.memset(wa_st_f[0:32, :], 0.0))
        p.op("pool", lambda e: e.memset(wa_st_b[0:32, :], 0.0))
        p.op("pool", lambda e: e.memset(ones_f, 1.0))
        p.op("pool", lambda e: e.memset(ones_bf, 1.0))
        p.barrier()
        DMA(wa_st_f[0:16, :], wa_f_d, key="const")
        DMA(wa_st_b[16:32, :], wa_b_d, key="const")
        p.barrier()
        p.op("dve", lambda e: e.tensor_copy(out=wa_f_bf[0:32, :], in_=wa_st_f[0:32, :]))
        p.op("dve", lambda e: e.tensor_copy(out=wa_b_bf[0:32, :], in_=wa_st_b[0:32, :]))
        p.op("dve", lambda e: e.tensor_scalar(out=nba_f, in0=nba_f, scalar1=-1.0, scalar2=None, op0=ALU.mult))
        p.op("dve", lambda e: e.tensor_scalar(out=nba_b, in0=nba_b, scalar1=-1.0, scalar2=None, op0=ALU.mult))
        p.barrier()

        stage('const')
        AR.off = X0
        NSL = 6
        PW = 2048
        st_f = [AR.f32(PW) for _ in range(NSL)]
        st_b = [AR.bf(PW) for _ in range(NSL)]
        rot_b = [AR.bf(512) for _ in range(NSL)]
        T_sf = [T() for _ in range(NSL)]
        T_sb = [T() for _ in range(NSL)]
        T_rb = [T() for _ in range(NSL)]
        pc = [0]
        cast_engs = ("dve", "act")

        pieces = []

        def conv_piece(src, dst, gain, rot=None):
            pieces.append((src, dst, gain, rot))

        def piece_load(i):
            src, dst, gain, rot = pieces[i]
            sl = i % NSL
            n = src.shape[1]
            DMA(st_f[sl][:, 0:n], src, writes=[T_sf[sl]], key="pin%d" % sl)

        def flush_pieces(depth=4):
            n_ = len(pieces)
            for i in range(min(depth, n_)):
                piece_load(i)
            for i in range(n_):
                if i + depth < n_:
                    piece_load(i + depth)
                piece_rest(i)
            del pieces[:]

        def piece_rest(i):
            src, dst, gain, rot = pieces[i]
            sl = i % NSL
            n = src.shape[1]
            sf, sb = st_f[sl][:, 0:n], st_b[sl][:, 0:n]
            eng = cast_engs[i % 2]
            if gain is None:
                CP(eng, sb, sf, reads=[T_sf[sl]], writes=[T_sb[sl]])
            elif eng == "act":
                ACT(sb, sf, AF.Copy, reads=[T_sf[sl]], writes=[T_sb[sl]], scale=gain)
            else:
                TS(eng, sb, sf, gain, None, ALU.mult, reads=[T_sf[sl]], writes=[T_sb[sl]])
            if rot is not None:
                rot(sf, sl, gain)
            DMA(dst, sb, reads=[T_sb[sl]], key="pout%d" % sl)

        def conv_weight(w_d, s_d, nkc, ncols, gain_ap, rotf=None):
            for kc in range(nkc):
                g = None if gain_ap is None else gain_ap[:, kc:kc + 1]
                for c0 in range(0, ncols, PW):
                    c1 = min(ncols, c0 + PW)
                    rf = None
                    if rotf is not None:
                        rf = rotf(kc, c0, c1)
                    conv_piece(w_d[kc * 128:(kc + 1) * 128, c0:c1], s_d[:, kc, c0:c1], g, rf)

        def win_rot(kc, c0, c1):
            if not (c0 <= OFF_KR and OFF_KR + 64 <= c1):
                return None

            def f(sf, sl, gain):
                a = OFF_KR - c0
                rb = rot_b[sl]
                p.op("dve", lambda e: e.tensor_scalar(out=rb[:, 0:32], in0=sf[:, a + 32:a + 64], scalar1=gain, scalar2=-1.0,
                                                      op0=ALU.mult, op1=ALU.mult), [T_sf[sl]], [T_rb[sl]])
                p.op("dve", lambda e: e.tensor_scalar(out=rb[:, 32:64], in0=sf[:, a:a + 32], scalar1=gain, scalar2=None,
                                                      op0=ALU.mult), [T_sf[sl]], [T_rb[sl]])
                DMA(s_win[:, kc, OFF_KRROT:OFF_KRROT + 64], rb[:, 0:64], reads=[T_rb[sl]], key="prot%d" % sl)
            return f

        def wuq_rot(kc, c0, c1):
            def f(sf, sl, gain):
                rb = r3(rot_b[sl][:, 0:512], 64)
                s3 = r3(sf[:, 0:1536], 192)
                p.op("dve", lambda e: e.tensor_scalar(out=rb[:, :, 0:32], in0=s3[:, :, 160:192], scalar1=gain, scalar2=-1.0,
                                                      op0=ALU.mult, op1=ALU.mult), [T_sf[sl]], [T_rb[sl]])
                p.op("dve", lambda e: e.tensor_scalar(out=rb[:, :, 32:64], in0=s3[:, :, 128:160], scalar1=gain, scalar2=None,
                                                      op0=ALU.mult), [T_sf[sl]], [T_rb[sl]])
                DMA(s_wuq[:, kc, 1536:2048], rot_b[sl][:, 0:512], reads=[T_rb[sl]], key="prot%d" % sl)
            return f

        conv_weight(w_in_d, s_win, 8, DIN, g_pre, win_rot)
        conv_weight(w_uq_d, s_wuq, 2, 1536, g_q, wuq_rot)
        conv_weight(w_uk_d, s_wuk, 2, 1024, g_kv)
        conv_weight(w_uv_d, s_wuv, 2, 1024, g_kv)
        conv_weight(w_om_d, s_wom, 8, D, None)
        conv_weight(w_out_d, s_wout, 8, D, None)
        conv_weight(w_g_d, s_wg, 8, DFF, g_ffn)
        conv_weight(w_u_d, s_wu, 8, DFF, g_ffn)
        conv_weight(w_d_d, s_wd, NF, D, None)
        flush_pieces()
        p.barrier()
        for kc in range(8):
            conv_piece(w_og_d[kc * 128:(kc + 1) * 128, 0:D], s_wog[:, kc, 0:D], g_gla[:, (kc % 2):(kc % 2) + 1])
        flush_pieces()
        p.barrier()

        stage('prologue')
        def rstd_from_ss(ss, t_ss, n):
            r, t_r = stat()
            ACT(r, ss, AF.Ln, reads=[t_ss], writes=[t_r], scale=1.0 / n, bias=EPS)
            ACT(r, r, AF.Exp, reads=[t_r], writes=[t_r], scale=-0.5)
            return r, t_r

        def norm_transpose(src, t_src, dst3, t_dst, slot, xs_sl, T_xs, T_ptr, ncol=D, bt=None):
            ss, t_ss = stat()
            xs = xs_sl[slot]
            ACT(xs[:, 0:ncol], src, AF.Square, reads=[t_src], writes=[t_ss, T_xs[slot]], accum_out=ss)
            r, t_r = rstd_from_ss(ss, t_ss, ncol)
            ACT(xs[:, 0:ncol], src, AF.Copy, reads=[t_src, t_r], writes=[T_xs[slot]], scale=r)
            nk = ncol // 128
            pb = bank_bf(6 + slot if bt is None else bt)
            TR([(pb[:, k * 128:(k + 1) * 128], xs[:, k * 128:(k + 1) * 128]) for k in range(nk)], ident,
               reads=[T_xs[slot]], writes=[T_ptr[slot]])
            CP("dve", dst3, r3(pb[:, 0:ncol], 128), reads=[T_ptr[slot]], writes=[t_dst])

        def norm_part(src, t_src, xs, t_xs, ncol=D):
            ss, t_ss = stat()
            ACT(xs[:, 0:ncol], src, AF.Square, reads=[t_src], writes=[t_ss, t_xs], accum_out=ss)
            r, t_r = rstd_from_ss(ss, t_ss, ncol)
            ACT(xs[:, 0:ncol], src, AF.Copy, reads=[t_src, t_r], writes=[t_xs], scale=r)

        def tr_part(xs, t_xs, dst3, t_dst, bt, ncol=D):
            nk = ncol // 128
            pb = bank_bf(bt)
            TR([(pb[:, k * 128:(k + 1) * 128], xs[:, k * 128:(k + 1) * 128]) for k in range(nk)], ident, reads=[t_xs])
            CP("dve", dst3, r3(pb[:, 0:ncol], 128), writes=[t_dst])

        def wslice(s_d, c0, n):
            return s_d[:, :, c0:c0 + n]

        for s in range(nseq):
            tok0 = s * L
            AR.off = X0
            xsl = [AR.f32(D) for _ in range(3)]
            T_xsl = [T() for _ in range(3)]
            xs_sl = [AR.bf(D) for _ in range(2)]
            T_xs = [T() for _ in range(2)]
            T_ptr = [T() for _ in range(2)]
            for t in range(NT):
                sl = t % 3
                DMA(xsl[sl], x_d[tok0 + t * 128:tok0 + (t + 1) * 128, :], writes=[T_xsl[sl]], key="x%d" % sl)
                norm_transpose(xsl[sl], T_xsl[sl], hT[:, :, t * 128:(t + 1) * 128], T_hT[t], t % 2, xs_sl, T_xs, T_ptr)
            p.barrier()
            if dbg and s == 0:
                DMA(dbg_out["d_hT"], hT.rearrange("p a b -> p (a b)"), reads=T_hT, key="dbg")

            stage('p0')
            AR.off = X0
            afab = AR.bf(L)
            T_afab = [T() for _ in range(4)]
            w_afab = r3(AR.bf(8 * 32), 32)
            T_wafab = T()
            qT = AR.bf(L); kT = AR.bf(L)
            T_qT = [T() for _ in range(4)]; T_kT = [T() for _ in range(4)]
            cf32 = CT.rearrange("p a b -> p (a b)").bitcast(F32)
            T1s = [AR.f32(1024), cf32[:, 0:1024]]; T2s = [AR.f32(1024), cf32[:, 1024:2048]]
            T_T1s = [[T() for _ in range(8)] for _ in range(2)]; T_T2s = [[T() for _ in range(8)] for _ in range(2)]
            qdec = [AR.bf(L) for _ in range(2)]
            kinv = [AR.bf(L) for _ in range(2)]
            T_qdec = [[T() for _ in range(2)] for _ in range(2)]
            T_kinv = [[T() for _ in range(2)] for _ in range(2)]
            dec = [AR.f32(16) for _ in range(2)]
            T_dec = [[T() for _ in range(2)] for _ in range(2)]
            vh = r3(AR.bf(NT * 256), 256)
            T_vh = [T() for _ in range(NT)]
            opart = r3(AR.f32(NT * 256), 256)
            T_op = [T() for _ in range(NT)]
            Rst = [AR.f32(256) for _ in range(2)]
            T_R = [T() for _ in range(2)]
            Sbf = [AR.bf(256) for _ in range(2)]
            T_S = [T() for _ in range(2)]
            ATs = [[AR.bf(128) for _ in range(2)] for _ in range(2)]
            T_AT = [[T() for _ in range(2)] for _ in range(2)]
            ktok = [[AR.bf(128) for _ in range(2)] for _ in range(2)]
            T_ktok = [[T() for _ in range(2)] for _ in range(2)]
            ofin = [AR.f32(256) for _ in range(2)]
            T_ofin = [T() for _ in range(2)]
            gh = r3(AR.bf(NT * 256), 256)
            T_gh = [T() for _ in range(NT)]
            ogs = [AR.bf(256) for _ in range(2)]
            T_ogs = [T() for _ in range(2)]
            junkf = AR.f32(256)
            T_junk = T()
            zt = [AR.f32(512) for _ in range(2)]
            T_zt = [T() for _ in range(2)]
            wq = r3(AR.bf(8 * 128), 128); wk = r3(AR.bf(8 * 128), 128)
            wvg = r3(AR.bf(8 * 512), 512)
            wv = wvg[:, :, 0:256]; wgg = wvg[:, :, 256:512]
            T_wq, T_wk, T_wv, T_wgg = T(), T(), T(), T()
            T_pb = [T() for _ in range(8)]
            T_psc = [[T() for _ in range(2)] for _ in range(2)]
            T_po = [[T() for _ in range(2)] for _ in range(2)]
            T_pu = [[T() for _ in range(2)] for _ in range(2)]
            T_ptk = [[T() for _ in range(2)] for _ in range(2)]
            T_pg = [T() for _ in range(2)]
            T_pog = [T() for _ in range(2)]

            DMA(w_afab, wslice(s_win, OFF_AF, 32), writes=[T_wafab], key="wafab")
            for blk in range(4):
                pb = bank(blk % 2)
                MM(pb[0:32, :], [(w_afab[:, kc, :], hT[:, kc, blk * 512:(blk + 1) * 512]) for kc in range(KC)],
                   reads=[T_wafab] + T_hT[4 * blk:4 * blk + 4], writes=[T_pb[blk % 2]])
                CP("dve", afab[0:32, blk * 512:(blk + 1) * 512], pb[0:32, :], reads=[T_pb[blk % 2]], writes=[T_afab[blk]])

            stage('G_afab')
            def load_head_g(hh):
                DMA(wq, wslice(s_win, OFF_Q + hh * 128, 128), writes=[T_wq], key="wq")
                DMA(wk, wslice(s_win, OFF_K + hh * 128, 128), writes=[T_wk], key="wk")
                DMA(wv, wslice(s_win, OFF_V + hh * 256, 256), writes=[T_wv], key="wv")
                DMA(wgg, wslice(s_win, OFF_G + hh * 256, 256), writes=[T_wgg], key="wgg")
            load_head_g(0)
            for h in range(4):
                for blk in range(4):
                    hb = [hT[:, kc, blk * 512:(blk + 1) * 512] for kc in range(KC)]
                    rd = T_hT[4 * blk:4 * blk + 4]
                    MM(bank(0), [(wq[:, kc, :], hb[kc]) for kc in range(KC)], reads=[T_wq] + rd, writes=[T_pb[0]])
                    ACT(qT[:, blk * 512:(blk + 1) * 512], bank(0), AF.Copy, reads=[T_pb[0]], writes=[T_qT[blk]], scale=128.0 ** -0.5)
                    MM(bank(1), [(wk[:, kc, :], hb[kc]) for kc in range(KC)], reads=[T_wk] + rd, writes=[T_pb[1]])
                    CP("dve", kT[:, blk * 512:(blk + 1) * 512], bank(1), reads=[T_pb[1]], writes=[T_kT[blk]])
                def vg_tiles(t0_, t1_):
                    for t in range(t0_, t1_):
                        b = 2 + (t % 2)
                        MM(bank(b), [(hT[:, kc, t * 128:(t + 1) * 128], wvg[:, kc, :]) for kc in range(KC)],
                           reads=[T_wv, T_wgg, T_hT[t]], writes=[T_pb[b]])
                        CP("act", vh[:, t, :], bank(b, 0, 256), reads=[T_pb[b]], writes=[T_vh[t]])
                        ACT(gh[:, t, :], bank(b, 256, 512), AF.Silu, reads=[T_pb[b]], writes=[T_gh[t]])

                def stageA(sti):
                    d, hf = sti // 2, sti % 2
                    wa = wa_f_bf if d == 0 else wa_b_bf
                    nba = nba_f if d == 0 else nba_b
                    T1, T2, T_T1, T_T2 = T1s[sti % 2], T2s[sti % 2], T_T1s[sti % 2], T_T2s[sti % 2]
                    for b2 in range(2):
                        blk = hf * 2 + b2
                        pb = 4 + (blk % 2)
                        MM(bank(pb), [(wa[0:32, h * 128:(h + 1) * 128], afab[0:32, blk * 512:(blk + 1) * 512])],
                           reads=[T_afab[blk]], writes=[T_pb[pb]])
                        z = zt[blk % 2]
                        ACT(z, bank(pb), AF.Exp, reads=[T_pb[pb]], writes=[T_zt[blk % 2]], scale=-1.0, bias=nba[:, h:h + 1])
                        tw = T_T1[b2 * 4:(b2 + 1) * 4]
                        ACT(T1[:, b2 * 512:(b2 + 1) * 512], z, AF.Ln, reads=[T_zt[blk % 2]], writes=tw, bias=1.0)
                    vg_tiles(sti * 4, sti * 4 + 4)
                    for c in range(8):
                        cs = slice(c * 128, (c + 1) * 128)
                        T1c, T2c = T1[:, cs], T2[:, cs]
                        p.op("dve", (lambda T1c, T2c: lambda e: e.tensor_tensor_scan(out=T2c, data0=ones_f, data1=T1c, initial=0.0,
                                                                                  op0=ALU.mult, op1=ALU.add))(T1c, T2c),
                             reads=[T_T1[c]], writes=[T_T2[c]])
                        if d == 1:
                            STT(T1c, T1c, T2[:, c * 128 + 127:c * 128 + 128], T2c, ALU.add, ALU.subtract,
                                reads=[T_T1[c], T_T2[c]], writes=[T_T1[c]])

                def stageB(sti):
                    d, hf = sti // 2, sti % 2
                    T1, T2, T_T1, T_T2 = T1s[sti % 2], T2s[sti % 2], T_T1s[sti % 2], T_T2s[sti % 2]
                    src, T_src = (T2, T_T2) if d == 0 else (T1, T_T1)
                    oth, T_oth = (T1, T_T1) if d == 0 else (T2, T_T2)
                    hs = slice(hf * 1024, (hf + 1) * 1024)
                    if d == 0:
                        ACT(oth, src, AF.Exp, reads=T_src, writes=T_oth, scale=1.0 / 16)
                        ACT(src, src, AF.Exp, reads=T_src, writes=T_src, scale=-1.0 / 16)
                        E, T_E, EI, T_EI = src, T_src, oth, T_oth
                    else:
                        ACT(oth, src, AF.Exp, reads=T_src, writes=T_oth, scale=-1.0 / 16)
                        ACT(src, src, AF.Exp, reads=T_src, writes=T_src, scale=1.0 / 16)
                        E, T_E, EI, T_EI = oth, T_oth, src, T_src
                    TT("dve", qdec[d][:, hs], qT[:, hs], E, ALU.mult, reads=T_qT[2 * hf:2 * hf + 2] + T_E, writes=[T_qdec[d][hf]])
                    TT("pool", kinv[d][:, hs], kT[:, hs], EI, ALU.mult, reads=T_kT[2 * hf:2 * hf + 2] + T_EI, writes=[T_kinv[d][hf]])
                    e3 = r3(E, 128)
                    col = 127 if d == 0 else 0
                    CP("pool", r3(dec[d][:, hf * 8:(hf + 1) * 8], 1), e3[:, :, col:col + 1], reads=T_E, writes=[T_dec[d][hf]])

                stageA(0)
                for sti in range(4):
                    if sti + 1 < 4:
                        stageA(sti + 1)
                    stageB(sti)
                stage('G_dec')
                if h + 1 < 4:
                    load_head_g(h + 1)
                def cinfo(st):
                    out = []
                    for d in range(2):
                        c = (st, NT - 1 - st)[d]
                        csl = slice(c * 128, (c + 1) * 128)
                        out.append((c, c // 8, csl, qdec[d][:, csl], kinv[d][:, csl]))
                    return out

                def burst1(st):
                    sl = st % 2
                    info = cinfo(st)
                    pbf6 = bank_bf(6 + sl)
                    for d in range(2):
                        c, hf, csl, q_c, k_c = info[d]
                        MM(bank(0 + sl, d * 128, d * 128 + 128), [(k_c, q_c)], reads=[T_qdec[d][hf], T_kinv[d][hf]], writes=[T_psc[d][sl]])
                    for d in range(2):
                        c, hf, csl, q_c, k_c = info[d]
                        tks = pbf6[:, d * 128:(d + 1) * 128]
                        TR([(tks, k_c)], ident, reads=[T_kinv[d][hf]], writes=[T_ptk[d][sl]])
                    for d in range(2):
                        TT("dve", ATs[d][sl], bank(0 + sl, d * 128, d * 128 + 128), mask_f if d == 0 else mask_b, ALU.mult,
                           reads=[T_psc[d][sl]], writes=[T_AT[d][sl]])
                    for d in range(2):
                        tks = pbf6[:, d * 128:(d + 1) * 128]
                        CP("act", ktok[d][sl], tks, reads=[T_ptk[d][sl]], writes=[T_ktok[d][sl]])

                def burst2(st):
                    sl = st % 2
                    first = (st == 0)
                    info = cinfo(st)
                    for d in range(2):
                        c, hf, csl, q_c, k_c = info[d]
                        pu = bank(4 + sl, d * 256, d * 256 + 256)
                        MM(pu, [(ktok[d][sl], vh[:, c, :])], reads=[T_ktok[d][sl], T_vh[c]], writes=[T_pu[d][sl]])
                    for d in range(2):
                        c, hf, csl, q_c, k_c = info[d]
                        po = bank(2 + sl, d * 256, d * 256 + 256)
                        pairs = [(ATs[d][sl], vh[:, c, :])]
                        rds = [T_AT[d][sl], T_vh[c]]
                        if not first:
                            pairs.append((q_c, Sbf[d]))
                            rds += [T_qdec[d][hf], T_S[d]]
                        MM(po, pairs, reads=rds, writes=[T_po[d][sl]])
                    for d in range(2):
                        c, hf, csl, q_c, k_c = info[d]
                        pu = bank(4 + sl, d * 256, d * 256 + 256)
                        prev_c = c - 1 if d == 0 else c + 1
                        if first:
                            CP("dve", Rst[d], pu, reads=[T_pu[d][sl]], writes=[T_R[d]])
                        else:
                            STT(Rst[d], Rst[d], dec[d][:, prev_c:prev_c + 1], pu, ALU.mult, ALU.add,
                                reads=[T_R[d], T_pu[d][sl], T_dec[d][prev_c // 8]], writes=[T_R[d]])
                        if st < NT - 1:
                            ACT(Sbf[d], Rst[d], AF.Copy, reads=[T_R[d], T_dec[d][hf]], writes=[T_S[d]], scale=dec[d][:, c:c + 1])
                    for d in range(2):
                        c, hf, csl, q_c, k_c = info[d]
                        po = bank(2 + sl, d * 256, d * 256 + 256)
                        if st < NT // 2:
                            CP("dve", opart[:, c, :], po, reads=[T_po[d][sl]], writes=[T_op[c]])
                def finalize(st):
                    sl = st % 2
                    info = cinfo(st)
                    for d in range(2):
                        c, hf, csl, q_c, k_c = info[d]
                        po = bank(2 + sl, d * 256, d * 256 + 256)
                        TT("dve", ofin[d], po, opart[:, c, :], ALU.add, reads=[T_po[d][sl], T_op[c]], writes=[T_ofin[d]])
                    if True:
                        rr = []
                        for fs in range(2):
                            ss, t_ss = stat()
                            ACT(junkf, ofin[fs], AF.Square, reads=[T_ofin[fs]], writes=[t_ss, T_junk], accum_out=ss)
                            rr.append(rstd_from_ss(ss, t_ss, 256))
                        pbf = bank_bf(6 + sl)
                        for fs in range(2):
                            c = info[fs][0]
                            STT(ogs[fs], ofin[fs], rr[fs][0], gh[:, c, :], ALU.mult, ALU.mult, reads=[T_ofin[fs], rr[fs][1], T_gh[c]], writes=[T_ogs[fs]])
                        for fs in range(2):
                            o0 = 512 + fs * 256
                            TR([(pbf[:, o0 + j * 128:o0 + (j + 1) * 128], ogs[fs][:, j * 128:(j + 1) * 128]) for j in range(2)], ident,
                               reads=[T_ogs[fs]], writes=[T_pog[fs]])
                        for fs in range(2):
                            c, hf, csl, q_c, k_c = info[fs]
                            o0 = 512 + fs * 256
                            CP("dve", BT[:, 2 * h:2 * h + 2, csl], r3(pbf[:, o0:o0 + 256], 128), reads=[T_pog[fs]], writes=[T_og[h][c]])

                burst1(0)
                for st in range(NT):
                    if st + 1 < NT:
                        burst1(st + 1)
                    burst2(st)
                    if st - 1 >= NT // 2:
                        finalize(st - 1)
                finalize(NT - 1)
            p.barrier()
            if dbg and s == 0:
                DMA(dbg_out["d_ogT"], BT.rearrange("p a b -> p (a b)"), key="dbg")
                p.barrier()

            def gate_merge(off_gate, s_wproj, first, keyp):
                AR.off = X0
                wga = [r3(AR.bf(8 * 128), 128) for _ in range(2)]
                wpr = [r3(AR.bf(8 * 128), 128) for _ in range(2)]
                T_wga = [T() for _ in range(2)]
                T_wpr = [T() for _ in range(2)]
                sg = [AR.f32(512) for _ in range(2)]
                T_sgm = [T() for _ in range(2)]
                tm = [AR.f32(512) for _ in range(2)]
                T_tm = [T() for _ in range(2)]
                T_p1 = [T() for _ in range(2)]
                T_p2 = [T() for _ in range(2)]

                def load(m):
                    DMA(wga[m % 2], wslice(s_win, off_gate + m * 128, 128), writes=[T_wga[m % 2]], key=keyp + "g%d" % (m % 2))
                    DMA(wpr[m % 2], wslice(s_wproj, m * 128, 128), writes=[T_wpr[m % 2]], key=keyp + "p%d" % (m % 2))
                load(0)
                it = 0
                for m in range(8):
                    if m + 1 < 8:
                        load(m + 1)
                    for blk in range(4):
                        i2 = it % 2
                        it += 1
                        bs = slice(blk * 512, (blk + 1) * 512)
                        MM(bank(i2 * 2), [(wga[m % 2][:, kc, :], hT[:, kc, bs]) for kc in range(KC)],
                           reads=[T_wga[m % 2]] + T_hT[4 * blk:4 * blk + 4], writes=[T_p1[i2]])
                        if first:
                            rdb = [T_og[hh][c] for hh in range(4) for c in range(4 * blk, 4 * blk + 4)]
                        else:
                            rdb = [T_om[hh][blk] for hh in range(8)]
                        MM(bank(i2 * 2 + 1), [(wpr[m % 2][:, kc, :], BT[:, kc, bs]) for kc in range(KC)],
                           reads=[T_wpr[m % 2]] + rdb, writes=[T_p2[i2]])
                        ACT(sg[i2], bank(i2 * 2), AF.Sigmoid, reads=[T_p1[i2]], writes=[T_sgm[i2]])
                        if first:
                            TT("dve", CT[:, m, bs], sg[i2], bank(i2 * 2 + 1), ALU.mult, reads=[T_sgm[i2], T_p2[i2]], writes=[T_C[m][blk]])
                        else:
                            TT("dve", tm[i2], sg[i2], bank(i2 * 2 + 1), ALU.mult, reads=[T_sgm[i2], T_p2[i2]], writes=[T_tm[i2]])
                            TT("pool", CT[:, m, bs], CT[:, m, bs], tm[i2], ALU.add, reads=[T_tm[i2], T_C[m][blk]], writes=[T_C[m][blk]])
                p.barrier()

            stage('G')
            gate_merge(OFF_GA, s_wog, True, "ga")
            stage('ga')

            AR.off = X0
            cqT = r3(AR.bf(2 * L), L); ckvT = r3(AR.bf(2 * L), L)
            T_cq = [T() for _ in range(NT)]; T_ckv = [T() for _ in range(NT)]
            kpeT = AR.bf(L)
            T_kpe = [T() for _ in range(4)]
            p.op("pool", lambda e: e.memset(kpeT[64:128, :], 0.0), writes=T_kpe)
            cosT = AR.f32(L); sinT = AR.f32(L)
            T_cos, T_sin = T(), T()
            NPT = 6
            PTs = [AR.bf(512) for _ in range(NPT)]
            T_PT = [T() for _ in range(NPT)]
            t1 = AR.f32(512); t2 = AR.f32(512)
            T_t1, T_t2 = T(), T()
            rz = AR.f32(512)
            T_rz = T()
            zacc = [[AR.f32(512) for _ in range(2)] for _ in range(2)]
            T_zacc = [[T() for _ in range(2)] for _ in range(2)]
            hw_ = []
            for _ in range(2):
                hw_.append(dict(qn=r3(AR.bf(2 * 128), 128), qp=r3(AR.bf(2 * 64), 64), qr=r3(AR.bf(2 * 64), 64),
                                kn=r3(AR.bf(2 * 128), 128), vn=r3(AR.bf(2 * 128), 128),
                                T=dict(qn=T(), qp=T(), qr=T(), kn=T(), vn=T())))

            def head_bufs():
                hb0 = dict(qn=AR.bf(L), qp=AR.bf(L), kn=AR.bf(L), v=r3(AR.bf(NT * 128), 128),
                           T_qn=[T() for _ in range(4)], T_qp=[T() for _ in range(4)], T_kn=[T() for _ in range(4)],
                           T_v=[T() for _ in range(4)])
                qp_ = hb0["qp"]
                p.op("pool", lambda e: e.memset(qp_[64:128, :], 0.0), writes=hb0["T_qp"])
                return hb0
            hb_ = [head_bufs(), None]
            mark_m = AR.off
            wlat = r3(AR.bf(8 * 512), 512)
            wkr = r3(AR.bf(8 * 64), 64); wkrr = r3(AR.bf(8 * 64), 64)
            T_wlat, T_wkr, T_wkrr = T(), T(), T()
            cl = [AR.bf(512) for _ in range(2)]
            T_cl = [T() for _ in range(2)]
            junk2 = AR.f32(256)
            T_junk2 = T()

            DMA(cosT[0:64, :], cos_d, writes=[T_cos], key="cos")
            DMA(sinT[0:64, :], sin_d, writes=[T_sin], key="sin")
            DMA(wlat, wslice(s_win, OFF_CQ, 512), writes=[T_wlat], key="wlat")
            DMA(wkr, wslice(s_win, OFF_KR, 64), writes=[T_wkr], key="wkr")
            DMA(wkrr, wslice(s_win, OFF_KRROT, 64), writes=[T_wkrr], key="wkrr")
            for t in range(NT):
                b = t % 2
                ts_ = slice(t * 128, (t + 1) * 128)
                MM(bank(b), [(hT[:, kc, ts_], wlat[:, kc, :]) for kc in range(KC)], reads=[T_wlat, T_hT[t]])
                rr = []
                for j in range(2):
                    ss, t_ss = stat()
                    ACT(junk2, bank(b, j * 256, j * 256 + 256), AF.Square, writes=[t_ss, T_junk2], accum_out=ss)
                    rr.append(rstd_from_ss(ss, t_ss, 256))
                for j in range(2):
                    ACT(cl[b][:, j * 256:(j + 1) * 256], bank(b, j * 256, j * 256 + 256), AF.Copy, reads=[rr[j][1]],
                        writes=[T_cl[b]], scale=rr[j][0])
                pbf = bank_bf(6 + b)
                TR([(pbf[:, k * 128:(k + 1) * 128], cl[b][:, k * 128:(k + 1) * 128]) for k in range(4)], ident, reads=[T_cl[b]])
                CP("dve", cqT[:, :, ts_], r3(pbf[:, 0:256], 128), writes=[T_cq[t]])
                CP("dve", ckvT[:, :, ts_], r3(pbf[:, 256:512], 128), writes=[T_ckv[t]])

            def rope_a(pa, bs):
                TT("dve", t1[0:64, :], pa, cosT[0:64, bs], ALU.mult, reads=[T_cos], writes=[T_t1])

            def rope_b(pb_, dst, t_dst, bs):
                TT("dve", t2[0:64, :], pb_, sinT[0:64, bs], ALU.mult, reads=[T_sin], writes=[T_t2])
                TT("pool", dst, t1[0:64, :], t2[0:64, :], ALU.add, reads=[T_t1, T_t2], writes=[t_dst])

            for blk in range(4):
                bs = slice(blk * 512, (blk + 1) * 512)
                rd = T_hT[4 * blk:4 * blk + 4]
                MM(bank(2)[0:64, :], [(wkr[:, kc, :], hT[:, kc, bs]) for kc in range(KC)], reads=[T_wkr] + rd)
                rope_a(bank(2)[0:64, :], bs)
                MM(bank(3)[0:64, :], [(wkrr[:, kc, :], hT[:, kc, bs]) for kc in range(KC)], reads=[T_wkrr] + rd)
                rope_b(bank(3)[0:64, :], kpeT[0:64, bs], T_kpe[blk], bs)
            p.barrier()
            AR.off = mark_m
            hb_[1] = head_bufs()

            SCALE = 192.0 ** -0.5
            pbk = [0]

            def prep_bank():
                b = 6 + (pbk[0] % 2)
                pbk[0] += 1
                return b

            def load_head_w(h):
                w = hw_[h % 2]
                DMA(w["qn"], wslice(s_wuq, h * 192, 128), writes=[w["T"]["qn"]], key="wqn%d" % (h % 2))
                DMA(w["qp"], wslice(s_wuq, h * 192 + 128, 64), writes=[w["T"]["qp"]], key="wqp%d" % (h % 2))
                DMA(w["qr"], wslice(s_wuq, 1536 + h * 64, 64), writes=[w["T"]["qr"]], key="wqr%d" % (h % 2))
                DMA(w["kn"], wslice(s_wuk, h * 128, 128), writes=[w["T"]["kn"]], key="wkn%d" % (h % 2))
                DMA(w["vn"], wslice(s_wuv, h * 128, 128), writes=[w["T"]["vn"]], key="wvn%d" % (h % 2))

            def prep_groups(h):
                w = hw_[h % 2]
                hb = hb_[h % 2]
                gs = []
                for blk in range(4):
                    bs = slice(blk * 512, (blk + 1) * 512)
                    rq = T_cq[4 * blk:4 * blk + 4]
                    rk = T_ckv[4 * blk:4 * blk + 4]

                    def g_qn(bs=bs, rq=rq, blk=blk):
                        b = prep_bank()
                        MM(bank(b), [(w["qn"][:, kc, :], cqT[:, kc, bs]) for kc in range(2)], reads=[w["T"]["qn"]] + rq)
                        CP("act", hb["qn"][:, bs], bank(b), writes=[hb["T_qn"][blk]])

                    def g_kn(bs=bs, rk=rk, blk=blk):
                        b = prep_bank()
                        MM(bank(b), [(w["kn"][:, kc, :], ckvT[:, kc, bs]) for kc in range(2)], reads=[w["T"]["kn"]] + rk)
                        CP("act", hb["kn"][:, bs], bank(b), writes=[hb["T_kn"][blk]])

                    def g_qa(bs=bs, rq=rq, blk=blk):
                        b = prep_bank()
                        MM(bank(b)[0:64, :], [(w["qp"][:, kc, :], cqT[:, kc, bs]) for kc in range(2)], reads=[w["T"]["qp"]] + rq)
                        rope_a(bank(b)[0:64, :], bs)

                    def g_qb(bs=bs, rq=rq, blk=blk):
                        b = prep_bank()
                        MM(bank(b)[0:64, :], [(w["qr"][:, kc, :], cqT[:, kc, bs]) for kc in range(2)], reads=[w["T"]["qr"]] + rq)
                        rope_b(bank(b)[0:64, :], hb["qp"][0:64, bs], hb["T_qp"][blk], bs)

                    def g_v(blk=blk):
                        b = prep_bank()
                        pv = bank(b)
                        for tt in range(4):
                            t = blk * 4 + tt
                            MM(pv[:, tt * 128:(tt + 1) * 128], [(ckvT[:, kc, t * 128:(t + 1) * 128], w["vn"][:, kc, :]) for kc in range(2)],
                               reads=[w["T"]["vn"], T_ckv[t]])
                        CP("dve", hb["v"][:, blk * 4:(blk + 1) * 4, :], r3(pv, 128), writes=[hb["T_v"][blk]])
                    gs += [g_qn, g_kn, g_qa, g_qb, g_v]
                return gs

            load_head_w(0)
            for g in prep_groups(0):
                g()
            for h in range(8):
                hb = hb_[h % 2]
                pend = []
                if h + 1 < 8:
                    load_head_w(h + 1)
                    pend = prep_groups(h + 1)
                iters = [(qb, kt) for qb in range(4) for kt in range(NT)]

                def emit_S(i, hb=hb):
                    qb, kt = iters[i]
                    sl = i % 3
                    ps = i % NPT
                    qs = slice(qb * 512, (qb + 1) * 512)
                    ks = slice(kt * 128, (kt + 1) * 128)
                    MM(bank(sl), [(hb["kn"][:, ks], hb["qn"][:, qs]), (kpeT[:, ks], hb["qp"][:, qs])],
                       reads=[hb["T_kn"][kt // 4], hb["T_qn"][qb], T_kpe[kt // 4], hb["T_qp"][qb]])
                    ACT(PTs[ps], bank(sl), AF.Exp, writes=[T_PT[ps]], scale=SCALE)
                emit_S(0)
                emit_S(1)
                for i, (qb, kt) in enumerate(iters):
                    ps = i % NPT
                    ob = qb % 2

                    def pv_fn(e, kt=kt, ps=ps, ob=ob, hb=hb):
                        return e.matmul(bank(3 + ob), lhsT=hb["v"][:, kt, :], rhs=PTs[ps], start=(kt == 0), stop=(kt == NT - 1))
                    p.op("pe", pv_fn, reads=[T_PT[ps], hb["T_v"][kt // 4]], writes=[LOCK[3 + ob]])
                    ze = kt % 2
                    zeng = "dve" if ze == 0 else "pool"
                    if kt < 2:
                        CP(zeng, zacc[ze][ob], PTs[ps], reads=[T_PT[ps]], writes=[T_zacc[ze][ob]])
                    else:
                        TT(zeng, zacc[ze][ob], zacc[ze][ob], PTs[ps], ALU.add, reads=[T_PT[ps], T_zacc[ze][ob]], writes=[T_zacc[ze][ob]])
                    if i + 2 < len(iters):
                        emit_S(i + 2)
                    if pend and i % 3 == 2:
                        pend.pop(0)()
                    if kt == NT - 1:
                        qs = slice(qb * 512, (qb + 1) * 512)
                        MM(bank(5), [(ones_f, zacc[0][ob]), (ones_f, zacc[1][ob])], reads=[T_zacc[0][ob], T_zacc[1][ob]])
                        p.op("dve", lambda e: e.reciprocal(out=rz, in_=bank(5)), writes=[T_rz, LOCK[5]])
                        TT("dve", BT[:, h, qs], bank(3 + ob), rz, ALU.mult, reads=[T_rz], writes=[T_om[h][qb]])
                while pend:
                    pend.pop(0)()
            p.barrier()
            if dbg and s == 0:
                DMA(dbg_out["d_omT"], BT.rearrange("p a b -> p (a b)"), key="dbg")
                p.barrier()

            stage('M')
            gate_merge(OFF_GB, s_wom, False, "gb")
            stage('gb')
            if dbg and s == 0:
                DMA(dbg_out["d_C"], CT.rearrange("p a b -> p (a b)"), key="dbg")
                p.barrier()

            AR.off = X0
            FB = 4
            wout = r3(AR.bf(8 * D), D)
            T_wout = T()
            hT_f32 = hT.rearrange("p a b -> p (a b)").bitcast(F32)
            x1 = [r3(hT_f32[:, i * FB * D:(i + 1) * FB * D], D) for i in range(2)]
            T_x1 = [[T() for _ in range(FB)] for _ in range(2)]
            xr = [AR.f32(D) for _ in range(2)]
            T_xr = [T() for _ in range(2)]
            tmpn = [AR.f32(D) for _ in range(2)]
            T_tmp = [T() for _ in range(2)]
            tmpy, T_tmpy = tmpn, T_tmp
            xs2 = [AR.bf(D) for _ in range(FB)]
            T_xs2 = [T() for _ in range(FB)]
            h2T = r3(AR.bf(8 * 512), 512)
            T_h2 = [T() for _ in range(FB)]
            aT = r3(BT.rearrange("p a b -> p (a b)")[:, 0:NF * 512], 512)
            T_a = [T() for _ in range(NF)]
            NWS = 3
            wgs = [r3(AR.bf(8 * 128), 128) for _ in range(NWS)]
            wus = [r3(AR.bf(8 * 128), 128) for _ in range(NWS)]
            T_wgs = [T() for _ in range(NWS)]
            T_wus = [T() for _ in range(NWS)]
            bflat = BT.rearrange("p a b -> p (a b)")
            wds = [AR.bf(D) for _ in range(NF - 5)] + [bflat[:, NF * 512 + j * D:NF * 512 + (j + 1) * D] for j in range(5)]
            T_wds = [T() for _ in range(NF)]
            sgt = [AR.f32(512) for _ in range(2)]
            T_sgt = [T() for _ in range(2)]
            junk3 = AR.bf(512)
            T_junk3 = T()
            T_dummy = [T(), T()]
            DMA(wout, s_wout, writes=[T_wout], key="wout")
            rbk = [0]

            def nbank():
                b = rbk[0] % 8
                rbk[0] += 1
                return b

            def post_norm_parts(b0, b1):
                ssa, t_a = stat()
                ssb, t_b = stat()
                ACT(junk3, bank(b0), AF.Square, writes=[t_a, T_junk3], accum_out=ssa)
                ACT(junk3, bank(b1), AF.Square, writes=[t_b, T_junk3], accum_out=ssb)
                sst, t_s = stat()
                TT("dve", sst, ssa, ssb, ALU.add, reads=[t_a, t_b], writes=[t_s])
                return rstd_from_ss(sst, t_s, D)

            def p3_tiles(blk):
                xb = blk % 2
                for tt in range(FB):
                    t = blk * FB + tt
                    ts_ = slice(t * 128, (t + 1) * 128)
                    i2 = tt % 2
                    b0, b1 = nbank(), nbank()
                    rd = [T_C[m][blk] for m in range(8)] + [T_wout]
                    MM(bank(b0), [(CT[:, kc, ts_], wout[:, kc, 0:512]) for kc in range(KC)], reads=rd)
                    MM(bank(b1), [(CT[:, kc, ts_], wout[:, kc, 512:1024]) for kc in range(KC)], reads=rd)
                    DMA(xr[i2], x_d[tok0 + t * 128:tok0 + (t + 1) * 128, :], writes=[T_xr[i2]], key="xr%d" % i2)
                    r, t_r = post_norm_parts(b0, b1)
                    STT(tmpn[i2][:, 0:512], bank(b0), r, g_p1[:, 0:512], ALU.mult, ALU.mult, reads=[t_r], writes=[T_tmp[i2]])
                    STT(tmpn[i2][:, 512:1024], bank(b1), r, g_p1[:, 512:1024], ALU.mult, ALU.mult, reads=[t_r], writes=[T_tmp[i2]])
                    TT("pool", x1[xb][:, tt, :], tmpn[i2], xr[i2], ALU.add, reads=[T_tmp[i2], T_xr[i2]], writes=[T_x1[xb][tt]])
                    norm_part(x1[xb][:, tt, :], T_x1[xb][tt], xs2[tt], T_xs2[tt])

            def p3_trans(blk):
                for tt in range(FB):
                    tr_part(xs2[tt], T_xs2[tt], h2T[:, :, tt * 128:(tt + 1) * 128], T_h2[tt], nbank())

            def load_gu(f):
                DMA(wgs[f % NWS], wslice(s_wg, f * 128, 128), writes=[T_wgs[f % NWS]], key="wg%d" % (f % NWS))
                DMA(wus[f % NWS], wslice(s_wu, f * 128, 128), writes=[T_wus[f % NWS]], key="wu%d" % (f % NWS))

            for f in range(NF):
                DMA(wds[f], s_wd[:, f, :], writes=[T_wds[f]], key="wd%d" % f)

            def ffn1(blk):
                load_gu(0)
                load_gu(1)
                for f in range(NF):
                    if f + 2 < NF:
                        load_gu(f + 2)
                    i2 = f % 2
                    bg, bu = nbank(), nbank()
                    MM(bank(bg), [(wgs[f % NWS][:, kc, :], h2T[:, kc, :]) for kc in range(KC)], reads=[T_wgs[f % NWS]] + T_h2)
                    MM(bank(bu), [(wus[f % NWS][:, kc, :], h2T[:, kc, :]) for kc in range(KC)], reads=[T_wus[f % NWS]] + T_h2)
                    ACT(sgt[i2], bank(bg), AF.Silu, writes=[T_sgt[i2]])
                    TT("dve", aT[:, f, :], sgt[i2], bank(bu), ALU.mult, reads=[T_sgt[i2]], writes=[T_a[f]])

            def ffn2(blk):
                xb = blk % 2
                for pr in range(FB // 2):
                    bks = [nbank() for _ in range(4)]
                    for f in range(NF):
                        w = wds[f]

                        def dfn(e, f=f, w=w, pr=pr, bks=bks):
                            for j in range(2):
                                tt = pr * 2 + j
                                a = aT[:, f, tt * 128:(tt + 1) * 128]
                                e.matmul(bank(bks[2 * j]), lhsT=a, rhs=w[:, 0:512], start=(f == 0), stop=(f == NF - 1))
                                ins = e.matmul(bank(bks[2 * j + 1]), lhsT=a, rhs=w[:, 512:1024], start=(f == 0), stop=(f == NF - 1))
                            return ins
                        p.op("pe", dfn, reads=[T_a[f], T_wds[f]], writes=[LOCK[b] for b in bks])
                    for j in range(2):
                        tt = pr * 2 + j
                        t = blk * FB + tt
                        b0, b1 = bks[2 * j], bks[2 * j + 1]
                        r, t_r = post_norm_parts(b0, b1)
                        STT(tmpy[j][:, 0:512], bank(b0), r, g_p2[:, 0:512], ALU.mult, ALU.mult, reads=[t_r], writes=[T_tmpy[j]])
                        STT(tmpy[j][:, 512:1024], bank(b1), r, g_p2[:, 512:1024], ALU.mult, ALU.mult, reads=[t_r], writes=[T_tmpy[j]])
                        TT("pool", tmpy[j], tmpy[j], x1[xb][:, tt, :], ALU.add, reads=[T_tmpy[j], T_x1[xb][tt]], writes=[T_tmpy[j]])
                        DMA(y_d[tok0 + t * 128:tok0 + (t + 1) * 128, :], tmpy[j], reads=[T_tmpy[j]], key="y%d" % j, eng="pool")

            p3_tiles(0)
            p3_trans(0)
            for blk in range(4):
                ffn1(blk)
                if blk + 1 < 4:
                    p3_tiles(blk + 1)
                ffn2(blk)
                if blk + 1 < 4:
                    p3_trans(blk + 1)
            p.barrier()


    try:
        _build_body()
    except _Stop:
        p.barrier()
    p.emit()
    return nc, list(dbg_out.keys())


def _host_consts():
    c = {}
    c["ident"] = np.eye(128, dtype=np.float32).astype(ml_dtypes.bfloat16)
    inv = (10000.0 ** (-np.arange(0, 64, 2, dtype=np.float32) / np.float32(64))).astype(np.float32)
    ang = (np.arange(L, dtype=np.float32)[:, None] * inv[None, :]).astype(np.float32)
    cos = np.cos(ang).astype(np.float32).T
    sin = np.sin(ang).astype(np.float32).T
    c["cosT"] = np.ascontiguousarray(np.concatenate([cos, cos], axis=0))
    c["sinT"] = np.ascontiguousarray(np.concatenate([sin, sin], axis=0))
    j = np.arange(128)[:, None]
    i = np.arange(128)[None, :]
    c["mask_f"] = (i >= j).astype(np.float32)
    c["mask_b"] = (i <= j).astype(np.float32)
    return c


def _pk(v, n):
    return np.ascontiguousarray(np.asarray(v, np.float32).reshape(n, 128).T)


def make_in_maps(inputs, x_shards):
    f = lambda k: np.ascontiguousarray(np.asarray(inputs[k], np.float32)[0])
    common = dict(
        w_in=f("w_in"), w_o_gla=f("w_o_gla"), w_uq=f("w_uq"), w_uk=f("w_uk"), w_uv=f("w_uv"),
        w_o_mla=f("w_o_mla"), w_out=f("w_out"), w_gate=f("w_gate"), w_up=f("w_up"), w_down=f("w_down"),
        g_pre=_pk(f("norm_mix_pre"), 8), g_q=_pk(f("mla_norm_q"), 2), g_kv=_pk(f("mla_norm_kv"), 2),
        g_gla=_pk(f("gla_norm"), 2), g_ffn=_pk(f("norm_ffn_pre"), 8),
        g_post1=np.ascontiguousarray(np.broadcast_to(f("norm_mix_post")[None, :], (128, D))),
        g_post2=np.ascontiguousarray(np.broadcast_to(f("norm_ffn_post")[None, :], (128, D))),
        wa_f=f("gla_wa_fwd"), wa_b=f("gla_wa_bwd"),
        ba_f=_pk(f("gla_ba_fwd"), 4), ba_b=_pk(f("gla_ba_bwd"), 4),
    )
    common.update(_host_consts())
    return [dict(common, x=xs) for xs in x_shards]


def kernel(**inputs):
    xp = np.asarray(inputs["x_prompt"], np.float32)
    xs = np.asarray(inputs["x_sample"], np.float32)
    nb_p, nb_s = xp.shape[0], xs.shape[0]
    allx = np.concatenate([xp.reshape(nb_p, L, D), xs.reshape(nb_s, L, D)], axis=0)
    nseq_total = allx.shape[0]
    per = nseq_total // N_CORES
    shards = [np.ascontiguousarray(allx[c * per:(c + 1) * per].reshape(per * L, D)) for c in range(N_CORES)]
    nc, _ = build_program(per)
    in_maps = make_in_maps(inputs, shards)
    res = run_bass_kernel_spmd(nc, in_maps, core_ids=list(range(N_CORES)))
    y = np.concatenate([np.asarray(r["y"], np.float32).reshape(per, L, D) for r in res.results], axis=0)
    return (np.ascontiguousarray(y[:nb_p]), np.ascontiguousarray(y[nb_p:]))
```

```python
import contextlib
import numpy as np
import ml_dtypes
import concourse.bass as bass
import concourse.mybir as mybir
from concourse.bass_utils import run_bass_kernel_spmd

F32 = mybir.dt.float32
BF16 = mybir.dt.bfloat16
AF = mybir.ActivationFunctionType
ALU = mybir.AluOpType

L = 2048
NT = 16
D = 1024
KC = 8
DIN = 5728
DFF = 2816
NF = 22
EPS = 1e-6
OFF_Q, OFF_K, OFF_V, OFF_G, OFF_AF, OFF_CQ, OFF_CKV, OFF_KR, OFF_GA, OFF_GB, OFF_KRROT = (
    0, 512, 1024, 2048, 3072, 3104, 3360, 3616, 3680, 4704, 5728)
WIN_COLS = DIN + 64
WUQ_COLS = 1536 + 512
N_CORES = 8
SEQ_PER_CORE = 6
ARENA_WORDS = 53200


class T:
    __slots__ = ("writer", "readers")

    def __init__(self):
        self.writer = None
        self.readers = []


class Op:
    __slots__ = ("eng", "fn", "deps", "signal", "is_dma", "key", "val")

    def __init__(self, eng, fn, is_dma, key):
        self.eng, self.fn, self.is_dma, self.key = eng, fn, is_dma, key
        self.deps = []
        self.signal = False
        self.val = None


class Prog:
    ENGS = ("pe", "act", "dve", "pool", "sp")

    def __init__(self, nc):
        self.nc = nc
        self.ops = {e: [] for e in self.ENGS}
        self.last = {e: None for e in self.ENGS}
        self.dmas = []

    def op(self, eng, fn, reads=(), writes=(), dma=False, key=None):
        o = Op(eng, fn, dma, key)
        deps = {}
        for t in reads:
            if t.writer is not None:
                deps[id(t.writer)] = t.writer
        for t in writes:
            if t.writer is not None:
                deps[id(t.writer)] = t.writer
            for r in t.readers:
                deps[id(r)] = r
        for d in deps.values():
            if (not d.is_dma) and (not dma) and d.eng == "pe" and eng == "pe":
                continue
            d.signal = True
            o.deps.append(d)
        for t in reads:
            if not dma:
                t.readers = [r for r in t.readers if r.is_dma or r.eng != eng]
            t.readers.append(o)
        for t in writes:
            t.writer = o
            t.readers = []
        self.ops[eng].append(o)
        if dma:
            self.dmas.append(o)
        else:
            self.last[eng] = o
        return o

    def barrier(self):
        fr = [o for o in self.last.values() if o is not None] + self.dmas
        self.dmas = []
        for d in fr:
            d.signal = True
        for e in self.ENGS:
            o = Op(e, None, False, None)
            o.deps = list(fr)
            self.ops[e].append(o)

    def emit(self):
        nc = self.nc
        stack = contextlib.ExitStack()
        esem = {e: stack.enter_context(nc.semaphore("s_" + e)) for e in self.ENGS}
        dsem, dcount = {}, {}
        for e in self.ENGS:
            cnt = 0
            for o in self.ops[e]:
                if o.fn is None:
                    continue
                if o.is_dma:
                    k = o.key
                    if k not in dsem:
                        dsem[k] = stack.enter_context(nc.semaphore("d_" + str(k)))
                        dcount[k] = 0
                    dcount[k] += 16
                    o.val = (dsem[k], dcount[k])
                    o.signal = True
                elif o.signal:
                    cnt += 1
                    o.val = (esem[e], cnt)
        with stack:
            with nc.Block() as block:
                def mk(e):
                    def body(eng):
                        waited = {}
                        for o in self.ops[e]:
                            need = {}
                            for d in o.deps:
                                sem, v = d.val
                                if v > need.get(id(sem), (None, 0))[1]:
                                    need[id(sem)] = (sem, v)
                            for sem, v in need.values():
                                if waited.get(id(sem), 0) >= v:
                                    continue
                                waited[id(sem)] = v
                                eng.wait_ge(sem, v)
                            if o.fn is None:
                                continue
                            ins = o.fn(eng)
                            if o.signal:
                                ins.then_inc(o.val[0], 16 if o.is_dma else 1)
                    return body
                block.tensor(mk("pe"))
                block.scalar(mk("act"))
                block.vector(mk("dve"))
                block.gpsimd(mk("pool"))
                block.sync(mk("sp"))


class Arena:
    def __init__(self, ap, n):
        self.ap, self.n, self.off = ap, n, 0

    def f32(self, n):
        a = self.ap[:, self.off:self.off + n]
        self.off += n
        assert self.off <= self.n, ("arena overflow", self.off, self.n)
        return a

    def bf(self, n):
        w = (n + 1) // 2
        a = self.ap[:, self.off:self.off + w].bitcast(BF16)
        self.off += w
        assert self.off <= self.n, ("arena overflow", self.off, self.n)
        return a


def r3(ap, b):
    return ap.rearrange("p (a b) -> p a b", b=b)


class _Stop(Exception):
    pass


def build_program(nseq, dbg=False, stop=None):
    nc = bass.Bass("TRN2", target_bir_lowering=False)
    p = Prog(nc)
    ntok = nseq * L

    def stage(name):
        if stop == name:
            raise _Stop()

    def din(name, shape, dt=F32):
        return nc.dram_tensor(name, list(shape), dt, kind="ExternalInput").ap()

    def dscr(name, shape, dt=BF16):
        return nc.dram_tensor(name, list(shape), dt, kind="Internal").ap()

    x_d = din("x", [ntok, D])
    y_d = nc.dram_tensor("y", [ntok, D], F32, kind="ExternalOutput").ap()
    w_in_d = din("w_in", [D, DIN])
    w_og_d = din("w_o_gla", [D, D])
    w_uq_d = din("w_uq", [256, 1536])
    w_uk_d = din("w_uk", [256, 1024])
    w_uv_d = din("w_uv", [256, 1024])
    w_om_d = din("w_o_mla", [D, D])
    w_out_d = din("w_out", [D, D])
    w_g_d = din("w_gate", [D, DFF])
    w_u_d = din("w_up", [D, DFF])
    w_d_d = din("w_down", [DFF, D])
    g_pre_d = din("g_pre", [128, 8])
    g_q_d = din("g_q", [128, 2])
    g_kv_d = din("g_kv", [128, 2])
    g_gla_d = din("g_gla", [128, 2])
    g_ffn_d = din("g_ffn", [128, 8])
    g_p1_d = din("g_post1", [128, D])
    g_p2_d = din("g_post2", [128, D])
    wa_f_d = din("wa_f", [16, 512])
    wa_b_d = din("wa_b", [16, 512])
    ba_f_d = din("ba_f", [128, 4])
    ba_b_d = din("ba_b", [128, 4])
    ident_d = din("ident", [128, 128], BF16)
    cos_d = din("cosT", [64, L])
    sin_d = din("sinT", [64, L])
    mk_f_d = din("mask_f", [128, 128])
    mk_b_d = din("mask_b", [128, 128])

    s_win = dscr("s_win", [128, 8, WIN_COLS])
    s_wog = dscr("s_wog", [128, 8, D])
    s_wuq = dscr("s_wuq", [128, 2, WUQ_COLS])
    s_wuk = dscr("s_wuk", [128, 2, 1024])
    s_wuv = dscr("s_wuv", [128, 2, 1024])
    s_wom = dscr("s_wom", [128, 8, D])
    s_wout = dscr("s_wout", [128, 8, D])
    s_wg = dscr("s_wg", [128, 8, DFF])
    s_wu = dscr("s_wu", [128, 8, DFF])
    s_wd = dscr("s_wd", [128, NF, D])

    dbg_out = {}
    if dbg:
        for nm, shp in (("d_hT", [128, 8 * L]), ("d_ogT", [128, 8 * L]), ("d_omT", [128, 8 * L]),
                        ("d_C", [128, 8 * L])):
            dbg_out[nm] = nc.dram_tensor(nm, shp, BF16, kind="ExternalOutput").ap()

    arena_t = nc.alloc_sbuf_tensor("arena", [128, ARENA_WORDS], F32)
    psum_t = nc.alloc_psum_tensor("psum", [128, 4096], F32)
    AR = Arena(arena_t.ap(), ARENA_WORDS)
    PS = psum_t.ap()

    def bank(b, lo=0, hi=512):
        return PS[:, b * 512 + lo:b * 512 + hi]

    def bank_bf(b):
        return PS[:, b * 512:(b + 1) * 512].bitcast(BF16)

    LOCK = [T() for _ in range(8)]

    def pl(*aps):
        out = []
        for a in aps:
            if a is None or isinstance(a, (int, float)):
                continue
            if getattr(a, "name", None) == "psum":
                size = 4 if a.dtype == F32 else 2
                b = ((a.offset * size) % 16384) // 2048
                if LOCK[b] not in out:
                    out.append(LOCK[b])
        return out

    def DMA(out, in_, reads=(), writes=(), key=None, eng="sp"):
        return p.op(eng, lambda e: e.dma_start(out=out, in_=in_), reads, writes, dma=True, key=key)

    def ACT(out, in_, func, reads=(), writes=(), **kw):
        return p.op("act", lambda e: e.activation(out=out, in_=in_, func=func, **kw), reads, list(writes) + pl(out, in_))

    def TT(eng, out, in0, in1, op, reads=(), writes=()):
        return p.op(eng, lambda e: e.tensor_tensor(out=out, in0=in0, in1=in1, op=op), reads, list(writes) + pl(out, in0, in1))

    def TS(eng, out, in0, s1, s2, op0, op1=None, reads=(), writes=()):
        writes = list(writes) + pl(out, in0)
        if op1 is None:
            return p.op(eng, lambda e: e.tensor_scalar(out=out, in0=in0, scalar1=s1, scalar2=None, op0=op0), reads, writes)
        return p.op(eng, lambda e: e.tensor_scalar(out=out, in0=in0, scalar1=s1, scalar2=s2, op0=op0, op1=op1), reads, writes)

    def STT(out, in0, scalar, in1, op0, op1, reads=(), writes=()):
        return p.op("dve", lambda e: e.scalar_tensor_tensor(out=out, in0=in0, scalar=scalar, in1=in1, op0=op0, op1=op1), reads,
                    list(writes) + pl(out, in0, in1))

    def CP(eng, out, in_, reads=(), writes=()):
        writes = list(writes) + pl(out, in_)
        if eng == "act":
            return p.op("act", lambda e: e.activation(out=out, in_=in_, func=AF.Copy), reads, writes)
        return p.op(eng, lambda e: e.tensor_copy(out=out, in_=in_), reads, writes)

    def MM(out, pairs, reads=(), writes=()):
        pairs = list(pairs)
        writes = list(writes) + pl(out)

        def fn(e):
            n = len(pairs)
            for i, (l, r) in enumerate(pairs):
                ins = e.matmul(out, lhsT=l, rhs=r, start=(i == 0), stop=(i == n - 1))
            return ins
        return p.op("pe", fn, reads, writes)

    def TR(outs_ins, ident_ap, reads=(), writes=()):
        outs_ins = list(outs_ins)
        writes = list(writes) + pl(*[o for o, _ in outs_ins])

        def fn(e):
            for o, i in outs_ins:
                ins = e.transpose(out=o, in_=i, identity=ident_ap)
            return ins
        return p.op("pe", fn, reads, writes)

    ident = AR.bf(128)
    ones_bf = AR.bf(128)
    ones_f = AR.f32(128)
    mask_fb = AR.f32(256)
    mask_f = mask_fb[:, 0:128]
    mask_b = mask_fb[:, 128:256]
    g_pre = AR.f32(8)
    g_q = AR.f32(2)
    g_kv = AR.f32(2)
    g_gla = AR.f32(2)
    g_ffn = AR.f32(8)
    nba_f = AR.f32(4)
    nba_b = AR.f32(4)
    AR.off += 2
    wa_f_bf = AR.bf(512)
    wa_b_bf = AR.bf(512)
    g_p1 = AR.f32(D)
    g_p2 = AR.f32(D)
    NSTAT = 64
    stat_ap = AR.f32(NSTAT)
    stat_T = [T() for _ in range(NSTAT)]
    stat_i = [0]

    def stat():
        i = stat_i[0] % NSTAT
        stat_i[0] += 1
        return stat_ap[:, i:i + 1], stat_T[i]

    hT = r3(AR.bf(8 * L), L)
    BT = r3(AR.bf(8 * L), L)
    CT = r3(AR.bf(8 * L), L)
    T_hT = [T() for _ in range(NT)]
    T_og = [[T() for _ in range(NT)] for _ in range(4)]
    T_om = [[T() for _ in range(4)] for _ in range(8)]
    T_C = [[T() for _ in range(4)] for _ in range(8)]
    X0 = AR.off
    T_const = T()

    def _build_body():
        for dst, src in ((ident, ident_d), (mask_f, mk_f_d), (mask_b, mk_b_d), (g_pre, g_pre_d), (g_q, g_q_d),
                         (g_kv, g_kv_d), (g_gla, g_gla_d), (g_ffn, g_ffn_d), (g_p1, g_p1_d), (g_p2, g_p2_d),
                         (nba_f, ba_f_d), (nba_b, ba_b_d)):
            DMA(dst, src, key="const")
        wa_st_f = AR.f32(512)
        wa_st_b = AR.f32(512)
        p.op("pool", lambda e: e.memset(wa_st_f[0:32, :], 0.0))
        p.op("pool", lambda e: e.memset(wa_st_b[0:32, :], 0.0))
        p.op("pool", lambda e: e.memset(ones_f, 1.0))
        p.op("pool", lambda e: e.memset(ones_bf, 1.0))
        p.barrier()
        DMA(wa_st_f[0:16, :], wa_f_d, key="const")
        DMA(wa_st_b[16:32, :], wa_b_d, key="const")
        p.barrier()
        p.op("dve", lambda e: e.tensor_copy(out=wa_f_bf[0:32, :], in_=wa_st_f[0:32, :]))
        p.op("dve", lambda e: e.tensor_copy(out=wa_b_bf[0:32, :], in_=wa_st_b[0:32, :]))
        p.op("dve", lambda e: e.tensor_scalar(out=nba_f, in0=nba_f, scalar1=-1.0, scalar2=None, op0=ALU.mult))
        p.op("dve", lambda e: e.tensor_scalar(out=nba_b, in0=nba_b, scalar1=-1.0, scalar2=None, op0=ALU.mult))
        p.barrier()

        stage('const')
        AR.off = X0
        NSL = 6
        PW = 2048
        st_f = [AR.f32(PW) for _ in range(NSL)]
        st_b = [AR.bf(PW) for _ in range(NSL)]
        rot_b = [AR.bf(512) for _ in range(NSL)]
        T_sf = [T() for _ in range(NSL)]
        T_sb = [T() for _ in range(NSL)]
        T_rb = [T() for _ in range(NSL)]
        pc = [0]
        cast_engs = ("dve", "act")

        pieces = []

        def conv_piece(src, dst, gain, rot=None):
            pieces.append((src, dst, gain, rot))

        def piece_load(i):
            src, dst, gain, rot = pieces[i]
            sl = i % NSL
            n = src.shape[1]
            DMA(st_f[sl][:, 0:n], src, writes=[T_sf[sl]], key="pin%d" % sl)

        def flush_pieces(depth=4):
            n_ = len(pieces)
            for i in range(min(depth, n_)):
                piece_load(i)
            for i in range(n_):
                if i + depth < n_:
                    piece_load(i + depth)
                piece_rest(i)
            del pieces[:]

        def piece_rest(i):
            src, dst, gain, rot = pieces[i]
            sl = i % NSL
            n = src.shape[1]
            sf, sb = st_f[sl][:, 0:n], st_b[sl][:, 0:n]
            eng = cast_engs[i % 2]
            if gain is None:
                CP(eng, sb, sf, reads=[T_sf[sl]], writes=[T_sb[sl]])
            elif eng == "act":
                ACT(sb, sf, AF.Copy, reads=[T_sf[sl]], writes=[T_sb[sl]], scale=gain)
            else:
                TS(eng, sb, sf, gain, None, ALU.mult, reads=[T_sf[sl]], writes=[T_sb[sl]])
            if rot is not None:
                rot(sf, sl, gain)
            DMA(dst, sb, reads=[T_sb[sl]], key="pout%d" % sl)

        def conv_weight(w_d, s_d, nkc, ncols, gain_ap, rotf=None):
            for kc in range(nkc):
                g = None if gain_ap is None else gain_ap[:, kc:kc + 1]
                for c0 in range(0, ncols, PW):
                    c1 = min(ncols, c0 + PW)
                    rf = None
                    if rotf is not None:
                        rf = rotf(kc, c0, c1)
                    conv_piece(w_d[kc * 128:(kc + 1) * 128, c0:c1], s_d[:, kc, c0:c1], g, rf)

        def win_rot(kc, c0, c1):
            if not (c0 <= OFF_KR and OFF_KR + 64 <= c1):
                return None

            def f(sf, sl, gain):
                a = OFF_KR - c0
                rb = rot_b[sl]
                p.op("dve", lambda e: e.tensor_scalar(out=rb[:, 0:32], in0=sf[:, a + 32:a + 64], scalar1=gain, scalar2=-1.0,
                                                      op0=ALU.mult, op1=ALU.mult), [T_sf[sl]], [T_rb[sl]])
                p.op("dve", lambda e: e.tensor_scalar(out=rb[:, 32:64], in0=sf[:, a:a + 32], scalar1=gain, scalar2=None,
                                                      op0=ALU.mult), [T_sf[sl]], [T_rb[sl]])
                DMA(s_win[:, kc, OFF_KRROT:OFF_KRROT + 64], rb[:, 0:64], reads=[T_rb[sl]], key="prot%d" % sl)
            return f

        def wuq_rot(kc, c0, c1):
            def f(sf, sl, gain):
                rb = r3(rot_b[sl][:, 0:512], 64)
                s3 = r3(sf[:, 0:1536], 192)
                p.op("dve", lambda e: e.tensor_scalar(out=rb[:, :, 0:32], in0=s3[:, :, 160:192], scalar1=gain, scalar2=-1.0,
                                                      op0=ALU.mult, op1=ALU.mult), [T_sf[sl]], [T_rb[sl]])
                p.op("dve", lambda e: e.tensor_scalar(out=rb[:, :, 32:64], in0=s3[:, :, 128:160], scalar1=gain, scalar2=None,
                                                      op0=ALU.mult), [T_sf[sl]], [T_rb[sl]])
                DMA(s_wuq[:, kc, 1536:2048], rot_b[sl][:, 0:512], reads=[T_rb[sl]], key="prot%d" % sl)
            return f

        conv_weight(w_in_d, s_win, 8, DIN, g_pre, win_rot)
        conv_weight(w_uq_d, s_wuq, 2, 1536, g_q, wuq_rot)
        conv_weight(w_uk_d, s_wuk, 2, 1024, g_kv)
        conv_weight(w_uv_d, s_wuv, 2, 1024, g_kv)
        conv_weight(w_om_d, s_wom, 8, D, None)
        conv_weight(w_out_d, s_wout, 8, D, None)
        conv_weight(w_g_d, s_wg, 8, DFF, g_ffn)
        conv_weight(w_u_d, s_wu, 8, DFF, g_ffn)
        conv_weight(w_d_d, s_wd, NF, D, None)
        flush_pieces()
        p.barrier()
        for kc in range(8):
            conv_piece(w_og_d[kc * 128:(kc + 1) * 128, 0:D], s_wog[:, kc, 0:D], g_gla[:, (kc % 2):(kc % 2) + 1])
        flush_pieces()
        p.barrier()

        stage('prologue')
        def rstd_from_ss(ss, t_ss, n):
            r, t_r = stat()
            ACT(r, ss, AF.Ln, reads=[t_ss], writes=[t_r], scale=1.0 / n, bias=EPS)
            ACT(r, r, AF.Exp, reads=[t_r], writes=[t_r], scale=-0.5)
            return r, t_r

        def norm_transpose(src, t_src, dst3, t_dst, slot, xs_sl, T_xs, T_ptr, ncol=D, bt=None):
            ss, t_ss = stat()
            xs = xs_sl[slot]
            ACT(xs[:, 0:ncol], src, AF.Square, reads=[t_src], writes=[t_ss, T_xs[slot]], accum_out=ss)
            r, t_r = rstd_from_ss(ss, t_ss, ncol)
            TS("dve", xs[:, 0:ncol], src, r, None, ALU.mult, reads=[t_src, t_r], writes=[T_xs[slot]])
            nk = ncol // 128
            pb = bank_bf(6 + slot if bt is None else bt)
            TR([(pb[:, k * 128:(k + 1) * 128], xs[:, k * 128:(k + 1) * 128]) for k in range(nk)], ident,
               reads=[T_xs[slot]], writes=[T_ptr[slot]])
            CP("dve", dst3, r3(pb[:, 0:ncol], 128), reads=[T_ptr[slot]], writes=[t_dst])

        def norm_part(src, t_src, xs, t_xs, ncol=D):
            ss, t_ss = stat()
            ACT(xs[:, 0:ncol], src, AF.Square, reads=[t_src], writes=[t_ss, t_xs], accum_out=ss)
            r, t_r = rstd_from_ss(ss, t_ss, ncol)
            ACT(xs[:, 0:ncol], src, AF.Copy, reads=[t_src, t_r], writes=[t_xs], scale=r)

        def tr_part(xs, t_xs, dst3, t_dst, bt, ncol=D):
            nk = ncol // 128
            pb = bank_bf(bt)
            TR([(pb[:, k * 128:(k + 1) * 128], xs[:, k * 128:(k + 1) * 128]) for k in range(nk)], ident, reads=[t_xs])
            CP("dve", dst3, r3(pb[:, 0:ncol], 128), writes=[t_dst])

        def wslice(s_d, c0, n):
            return s_d[:, :, c0:c0 + n]

        for s in range(nseq):
            tok0 = s * L
            AR.off = X0
            xsl = [AR.f32(D) for _ in range(3)]
            T_xsl = [T() for _ in range(3)]
            xs_sl = [AR.bf(D) for _ in range(2)]
            T_xs = [T() for _ in range(2)]
            T_ptr = [T() for _ in range(2)]
            for t in range(NT):
                sl = t % 3
                DMA(xsl[sl], x_d[tok0 + t * 128:tok0 + (t + 1) * 128, :], writes=[T_xsl[sl]], key="x%d" % sl)
                norm_transpose(xsl[sl], T_xsl[sl], hT[:, :, t * 128:(t + 1) * 128], T_hT[t], t % 2, xs_sl, T_xs, T_ptr)
            p.barrier()
            if dbg and s == 0:
                DMA(dbg_out["d_hT"], hT.rearrange("p a b -> p (a b)"), reads=T_hT, key="dbg")

            stage('p0')
            AR.off = X0
            afab = AR.bf(L)
            T_afab = [T() for _ in range(4)]
            w_afab = r3(AR.bf(8 * 32), 32)
            T_wafab = T()
            qT = AR.bf(L); kT = AR.bf(L)
            T_qT = [T() for _ in range(4)]; T_kT = [T() for _ in range(4)]
            cf32 = CT.rearrange("p a b -> p (a b)").bitcast(F32)
            T1s = [AR.f32(1024), cf32[:, 0:1024]]; T2s = [AR.f32(1024), cf32[:, 1024:2048]]
            T_T1s = [[T() for _ in range(8)] for _ in range(2)]; T_T2s = [[T() for _ in range(8)] for _ in range(2)]
            qdec = [AR.bf(L) for _ in range(2)]
            kinv = [AR.bf(L) for _ in range(2)]
            T_qdec = [[T() for _ in range(2)] for _ in range(2)]
            T_kinv = [[T() for _ in range(2)] for _ in range(2)]
            dec = [AR.f32(16) for _ in range(2)]
            T_dec = [[T() for _ in range(2)] for _ in range(2)]
            vh = r3(AR.bf(NT * 256), 256)
            T_vh = [T() for _ in range(NT)]
            opart = r3(AR.f32(NT * 256), 256)
            T_op = [T() for _ in range(NT)]
            Rst = [AR.f32(256) for _ in range(2)]
            T_R = [T() for _ in range(2)]
            Sbf = [AR.bf(256) for _ in range(2)]
            T_S = [T() for _ in range(2)]
            AT2 = [AR.bf(256) for _ in range(2)]
            ATs = [[AT2[k][:, d_ * 128:(d_ + 1) * 128] for k in range(2)] for d_ in range(2)]
            T_AT = [[T() for _ in range(2)] for _ in range(2)]
            KT2 = [AR.bf(256) for _ in range(2)]
            ktok = [[KT2[k][:, d_ * 128:(d_ + 1) * 128] for k in range(2)] for d_ in range(2)]
            T_ktok = [[T() for _ in range(2)] for _ in range(2)]
            ofin = [AR.f32(256) for _ in range(2)]
            T_ofin = [T() for _ in range(2)]
            gh = r3(AR.bf(NT * 256), 256)
            T_gh = [T() for _ in range(NT)]
            ogs = [AR.bf(256) for _ in range(2)]
            T_ogs = [T() for _ in range(2)]
            junkf = AR.f32(256)
            T_junk = T()
            zt = [AR.f32(512) for _ in range(2)]
            T_zt = [T() for _ in range(2)]
            wq = r3(AR.bf(8 * 128), 128); wk = r3(AR.bf(8 * 128), 128)
            wvg = r3(AR.bf(8 * 512), 512)
            wv = wvg[:, :, 0:256]; wgg = wvg[:, :, 256:512]
            T_wq, T_wk, T_wv, T_wgg = T(), T(), T(), T()
            T_pb = [T() for _ in range(8)]
            T_psc = [[T() for _ in range(2)] for _ in range(2)]
            T_po = [[T() for _ in range(2)] for _ in range(2)]
            T_pu = [[T() for _ in range(2)] for _ in range(2)]
            T_ptk = [[T() for _ in range(2)] for _ in range(2)]
            T_pg = [T() for _ in range(2)]
            T_pog = [T() for _ in range(2)]

            DMA(w_afab, wslice(s_win, OFF_AF, 32), writes=[T_wafab], key="wafab")
            for blk in range(4):
                pb = bank(blk % 2)
                MM(pb[0:32, :], [(w_afab[:, kc, :], hT[:, kc, blk * 512:(blk + 1) * 512]) for kc in range(KC)],
                   reads=[T_wafab] + T_hT[4 * blk:4 * blk + 4], writes=[T_pb[blk % 2]])
                CP("dve", afab[0:32, blk * 512:(blk + 1) * 512], pb[0:32, :], reads=[T_pb[blk % 2]], writes=[T_afab[blk]])

            stage('G_afab')
            def load_head_g(hh):
                DMA(wq, wslice(s_win, OFF_Q + hh * 128, 128), writes=[T_wq], key="wq")
                DMA(wk, wslice(s_win, OFF_K + hh * 128, 128), writes=[T_wk], key="wk")
                DMA(wv, wslice(s_win, OFF_V + hh * 256, 256), writes=[T_wv], key="wv")
                DMA(wgg, wslice(s_win, OFF_G + hh * 256, 256), writes=[T_wgg], key="wgg")
            load_head_g(0)
            for h in range(4):
                for blk in range(4):
                    hb = [hT[:, kc, blk * 512:(blk + 1) * 512] for kc in range(KC)]
                    rd = T_hT[4 * blk:4 * blk + 4]
                    MM(bank(0), [(wq[:, kc, :], hb[kc]) for kc in range(KC)], reads=[T_wq] + rd, writes=[T_pb[0]])
                    ACT(qT[:, blk * 512:(blk + 1) * 512], bank(0), AF.Copy, reads=[T_pb[0]], writes=[T_qT[blk]], scale=128.0 ** -0.5)
                    MM(bank(1), [(wk[:, kc, :], hb[kc]) for kc in range(KC)], reads=[T_wk] + rd, writes=[T_pb[1]])
                    CP("dve", kT[:, blk * 512:(blk + 1) * 512], bank(1), reads=[T_pb[1]], writes=[T_kT[blk]])
                def vg_tiles(t0_, t1_):
                    for t in range(t0_, t1_):
                        b = 2 + (t % 2)
                        MM(bank(b), [(hT[:, kc, t * 128:(t + 1) * 128], wvg[:, kc, :]) for kc in range(KC)],
                           reads=[T_wv, T_wgg, T_hT[t]], writes=[T_pb[b]])
                        CP("act", vh[:, t, :], bank(b, 0, 256), reads=[T_pb[b]], writes=[T_vh[t]])
                        ACT(gh[:, t, :], bank(b, 256, 512), AF.Silu, reads=[T_pb[b]], writes=[T_gh[t]])

                def stageA(sti):
                    d, hf = sti // 2, sti % 2
                    wa = wa_f_bf if d == 0 else wa_b_bf
                    nba = nba_f if d == 0 else nba_b
                    T1, T2, T_T1, T_T2 = T1s[sti % 2], T2s[sti % 2], T_T1s[sti % 2], T_T2s[sti % 2]
                    for b2 in range(2):
                        blk = hf * 2 + b2
                        pb = 4 + (blk % 2)
                        MM(bank(pb), [(wa[0:32, h * 128:(h + 1) * 128], afab[0:32, blk * 512:(blk + 1) * 512])],
                           reads=[T_afab[blk]], writes=[T_pb[pb]])
                        z = zt[blk % 2]
                        ACT(z, bank(pb), AF.Exp, reads=[T_pb[pb]], writes=[T_zt[blk % 2]], scale=-1.0, bias=nba[:, h:h + 1])
                        tw = T_T1[b2 * 4:(b2 + 1) * 4]
                        ACT(T1[:, b2 * 512:(b2 + 1) * 512], z, AF.Ln, reads=[T_zt[blk % 2]], writes=tw, bias=1.0)
                    vg_tiles(sti * 4, sti * 4 + 4)
                    for c in range(8):
                        cs = slice(c * 128, (c + 1) * 128)
                        T1c, T2c = T1[:, cs], T2[:, cs]
                        p.op("dve", (lambda T1c, T2c: lambda e: e.tensor_tensor_scan(out=T2c, data0=ones_f, data1=T1c, initial=0.0,
                                                                                  op0=ALU.mult, op1=ALU.add))(T1c, T2c),
                             reads=[T_T1[c]], writes=[T_T2[c]])
                        if d == 1:
                            STT(T1c, T1c, T2[:, c * 128 + 127:c * 128 + 128], T2c, ALU.add, ALU.subtract,
                                reads=[T_T1[c], T_T2[c]], writes=[T_T1[c]])

                def stageB(sti):
                    d, hf = sti // 2, sti % 2
                    T1, T2, T_T1, T_T2 = T1s[sti % 2], T2s[sti % 2], T_T1s[sti % 2], T_T2s[sti % 2]
                    src, T_src = (T2, T_T2) if d == 0 else (T1, T_T1)
                    oth, T_oth = (T1, T_T1) if d == 0 else (T2, T_T2)
                    hs = slice(hf * 1024, (hf + 1) * 1024)
                    if d == 0:
                        ACT(oth, src, AF.Exp, reads=T_src, writes=T_oth, scale=1.0 / 16)
                        ACT(src, src, AF.Exp, reads=T_src, writes=T_src, scale=-1.0 / 16)
                        E, T_E, EI, T_EI = src, T_src, oth, T_oth
                    else:
                        ACT(oth, src, AF.Exp, reads=T_src, writes=T_oth, scale=-1.0 / 16)
                        ACT(src, src, AF.Exp, reads=T_src, writes=T_src, scale=1.0 / 16)
                        E, T_E, EI, T_EI = oth, T_oth, src, T_src
                    TT("dve", qdec[d][:, hs], qT[:, hs], E, ALU.mult, reads=T_qT[2 * hf:2 * hf + 2] + T_E, writes=[T_qdec[d][hf]])
                    TT("pool", kinv[d][:, hs], kT[:, hs], EI, ALU.mult, reads=T_kT[2 * hf:2 * hf + 2] + T_EI, writes=[T_kinv[d][hf]])
                    e3 = r3(E, 128)
                    col = 127 if d == 0 else 0
                    CP("pool", r3(dec[d][:, hf * 8:(hf + 1) * 8], 1), e3[:, :, col:col + 1], reads=T_E, writes=[T_dec[d][hf]])

                stageA(0)
                for sti in range(4):
                    if sti + 1 < 4:
                        stageA(sti + 1)
                    stageB(sti)
                stage('G_dec')
                if h + 1 < 4:
                    load_head_g(h + 1)
                def cinfo(st):
                    out = []
                    for d in range(2):
                        c = (st, NT - 1 - st)[d]
                        csl = slice(c * 128, (c + 1) * 128)
                        out.append((c, c // 8, csl, qdec[d][:, csl], kinv[d][:, csl]))
                    return out

                def burst1(st):
                    sl = st % 2
                    info = cinfo(st)
                    pbf6 = bank_bf(6 + sl)
                    for d in range(2):
                        c, hf, csl, q_c, k_c = info[d]
                        MM(bank(0 + sl, d * 128, d * 128 + 128), [(k_c, q_c)], reads=[T_qdec[d][hf], T_kinv[d][hf]], writes=[T_psc[d][sl]])
                    for d in range(2):
                        c, hf, csl, q_c, k_c = info[d]
                        tks = pbf6[:, d * 128:(d + 1) * 128]
                        TR([(tks, k_c)], ident, reads=[T_kinv[d][hf]], writes=[T_ptk[d][sl]])
                    TT("dve", AT2[sl], bank(0 + sl, 0, 256), mask_fb, ALU.mult,
                       reads=[T_psc[0][sl], T_psc[1][sl]], writes=[T_AT[0][sl], T_AT[1][sl]])
                    CP("act", KT2[sl], pbf6[:, 0:256], reads=[T_ptk[0][sl], T_ptk[1][sl]], writes=[T_ktok[0][sl], T_ktok[1][sl]])

                def burst2(st):
                    sl = st % 2
                    first = (st == 0)
                    info = cinfo(st)
                    for d in range(2):
                        c, hf, csl, q_c, k_c = info[d]
                        pu = bank(4 + sl, d * 256, d * 256 + 256)
                        MM(pu, [(ktok[d][sl], vh[:, c, :])], reads=[T_ktok[d][sl], T_vh[c]], writes=[T_pu[d][sl]])
                    for d in range(2):
                        c, hf, csl, q_c, k_c = info[d]
                        po = bank(2 + sl, d * 256, d * 256 + 256)
                        pairs = [(ATs[d][sl], vh[:, c, :])]
                        rds = [T_AT[d][sl], T_vh[c]]
                        if not first:
                            pairs.append((q_c, Sbf[d]))
                            rds += [T_qdec[d][hf], T_S[d]]
                        MM(po, pairs, reads=rds, writes=[T_po[d][sl]])
                    for d in range(2):
                        c, hf, csl, q_c, k_c = info[d]
                        pu = bank(4 + sl, d * 256, d * 256 + 256)
                        prev_c = c - 1 if d == 0 else c + 1
                        if first:
                            CP("dve", Rst[d], pu, reads=[T_pu[d][sl]], writes=[T_R[d]])
                        else:
                            STT(Rst[d], Rst[d], dec[d][:, prev_c:prev_c + 1], pu, ALU.mult, ALU.add,
                                reads=[T_R[d], T_pu[d][sl], T_dec[d][prev_c // 8]], writes=[T_R[d]])
                        if st < NT - 1:
                            ACT(Sbf[d], Rst[d], AF.Copy, reads=[T_R[d], T_dec[d][hf]], writes=[T_S[d]], scale=dec[d][:, c:c + 1])
                    for d in range(2):
                        c, hf, csl, q_c, k_c = info[d]
                        po = bank(2 + sl, d * 256, d * 256 + 256)
                        if st < NT // 2:
                            CP("dve", opart[:, c, :], po, reads=[T_po[d][sl]], writes=[T_op[c]])
                def finalize(st):
                    sl = st % 2
                    info = cinfo(st)
                    for d in range(2):
                        c, hf, csl, q_c, k_c = info[d]
                        po = bank(2 + sl, d * 256, d * 256 + 256)
                        TT("dve", ofin[d], po, opart[:, c, :], ALU.add, reads=[T_po[d][sl], T_op[c]], writes=[T_ofin[d]])
                    if True:
                        rr = []
                        for fs in range(2):
                            ss, t_ss = stat()
                            ACT(junkf, ofin[fs], AF.Square, reads=[T_ofin[fs]], writes=[t_ss, T_junk], accum_out=ss)
                            rr.append(rstd_from_ss(ss, t_ss, 256))
                        pbf = bank_bf(6 + sl)
                        for fs in range(2):
                            c = info[fs][0]
                            STT(ogs[fs], ofin[fs], rr[fs][0], gh[:, c, :], ALU.mult, ALU.mult, reads=[T_ofin[fs], rr[fs][1], T_gh[c]], writes=[T_ogs[fs]])
                        for fs in range(2):
                            o0 = 512 + fs * 256
                            TR([(pbf[:, o0 + j * 128:o0 + (j + 1) * 128], ogs[fs][:, j * 128:(j + 1) * 128]) for j in range(2)], ident,
                               reads=[T_ogs[fs]], writes=[T_pog[fs]])
                        for fs in range(2):
                            c, hf, csl, q_c, k_c = info[fs]
                            o0 = 512 + fs * 256
                            CP("dve", BT[:, 2 * h:2 * h + 2, csl], r3(pbf[:, o0:o0 + 256], 128), reads=[T_pog[fs]], writes=[T_og[h][c]])

                burst1(0)
                for st in range(NT):
                    if st + 1 < NT:
                        burst1(st + 1)
                    burst2(st)
                    if st - 1 >= NT // 2:
                        finalize(st - 1)
                finalize(NT - 1)
            p.barrier()
            if dbg and s == 0:
                DMA(dbg_out["d_ogT"], BT.rearrange("p a b -> p (a b)"), key="dbg")
                p.barrier()

            def gate_merge(off_gate, s_wproj, first, keyp):
                AR.off = X0
                wga = [r3(AR.bf(8 * 128), 128) for _ in range(2)]
                wpr = [r3(AR.bf(8 * 128), 128) for _ in range(2)]
                T_wga = [T() for _ in range(2)]
                T_wpr = [T() for _ in range(2)]
                sg = [AR.f32(512) for _ in range(2)]
                T_sgm = [T() for _ in range(2)]
                tm = [AR.f32(512) for _ in range(2)]
                T_tm = [T() for _ in range(2)]
                T_p1 = [T() for _ in range(2)]
                T_p2 = [T() for _ in range(2)]

                def load(m):
                    DMA(wga[m % 2], wslice(s_win, off_gate + m * 128, 128), writes=[T_wga[m % 2]], key=keyp + "g%d" % (m % 2))
                    DMA(wpr[m % 2], wslice(s_wproj, m * 128, 128), writes=[T_wpr[m % 2]], key=keyp + "p%d" % (m % 2))
                load(0)
                it = 0
                for m in range(8):
                    if m + 1 < 8:
                        load(m + 1)
                    for blk in range(4):
                        i2 = it % 2
                        it += 1
                        bs = slice(blk * 512, (blk + 1) * 512)
                        MM(bank(i2 * 2), [(wga[m % 2][:, kc, :], hT[:, kc, bs]) for kc in range(KC)],
                           reads=[T_wga[m % 2]] + T_hT[4 * blk:4 * blk + 4], writes=[T_p1[i2]])
                        if first:
                            rdb = [T_og[hh][c] for hh in range(4) for c in range(4 * blk, 4 * blk + 4)]
                        else:
                            rdb = [T_om[hh][blk] for hh in range(8)]
                        MM(bank(i2 * 2 + 1), [(wpr[m % 2][:, kc, :], BT[:, kc, bs]) for kc in range(KC)],
                           reads=[T_wpr[m % 2]] + rdb, writes=[T_p2[i2]])
                        ACT(sg[i2], bank(i2 * 2), AF.Sigmoid, reads=[T_p1[i2]], writes=[T_sgm[i2]])
                        if first:
                            TT("dve", CT[:, m, bs], sg[i2], bank(i2 * 2 + 1), ALU.mult, reads=[T_sgm[i2], T_p2[i2]], writes=[T_C[m][blk]])
                        else:
                            TT("dve", tm[i2], sg[i2], bank(i2 * 2 + 1), ALU.mult, reads=[T_sgm[i2], T_p2[i2]], writes=[T_tm[i2]])
                            TT("pool", CT[:, m, bs], CT[:, m, bs], tm[i2], ALU.add, reads=[T_tm[i2], T_C[m][blk]], writes=[T_C[m][blk]])
                p.barrier()

            stage('G')
            gate_merge(OFF_GA, s_wog, True, "ga")
            stage('ga')

            AR.off = X0
            cqT = r3(AR.bf(2 * L), L); ckvT = r3(AR.bf(2 * L), L)
            T_cq = [T() for _ in range(NT)]; T_ckv = [T() for _ in range(NT)]
            kpeT = AR.bf(L)
            T_kpe = [T() for _ in range(4)]
            p.op("pool", lambda e: e.memset(kpeT[64:128, :], 0.0), writes=T_kpe)
            cosT = AR.f32(L); sinT = AR.f32(L)
            T_cos, T_sin = T(), T()
            NPT = 6
            PTs = [AR.bf(512) for _ in range(NPT)]
            T_PT = [T() for _ in range(NPT)]
            t1 = AR.f32(512); t2 = AR.f32(512)
            T_t1, T_t2 = T(), T()
            rz = AR.f32(512)
            T_rz = T()
            zacc = [[AR.f32(512) for _ in range(2)] for _ in range(2)]
            T_zacc = [[T() for _ in range(2)] for _ in range(2)]
            hw_ = []
            for _ in range(2):
                hw_.append(dict(qn=r3(AR.bf(2 * 128), 128), qp=r3(AR.bf(2 * 64), 64), qr=r3(AR.bf(2 * 64), 64),
                                kn=r3(AR.bf(2 * 128), 128), vn=r3(AR.bf(2 * 128), 128),
                                T=dict(qn=T(), qp=T(), qr=T(), kn=T(), vn=T())))

            def head_bufs():
                hb0 = dict(qn=AR.bf(L), qp=AR.bf(L), kn=AR.bf(L), v=r3(AR.bf(NT * 128), 128),
                           T_qn=[T() for _ in range(4)], T_qp=[T() for _ in range(4)], T_kn=[T() for _ in range(4)],
                           T_v=[T() for _ in range(4)])
                qp_ = hb0["qp"]
                p.op("pool", lambda e: e.memset(qp_[64:128, :], 0.0), writes=hb0["T_qp"])
                return hb0
            hb_ = [head_bufs(), None]
            mark_m = AR.off
            wlat = r3(AR.bf(8 * 512), 512)
            wkr = r3(AR.bf(8 * 64), 64); wkrr = r3(AR.bf(8 * 64), 64)
            T_wlat, T_wkr, T_wkrr = T(), T(), T()
            cl = [AR.bf(512) for _ in range(2)]
            T_cl = [T() for _ in range(2)]
            junk2 = AR.f32(256)
            T_junk2 = T()

            DMA(cosT[0:64, :], cos_d, writes=[T_cos], key="cos")
            DMA(sinT[0:64, :], sin_d, writes=[T_sin], key="sin")
            DMA(wlat, wslice(s_win, OFF_CQ, 512), writes=[T_wlat], key="wlat")
            DMA(wkr, wslice(s_win, OFF_KR, 64), writes=[T_wkr], key="wkr")
            DMA(wkrr, wslice(s_win, OFF_KRROT, 64), writes=[T_wkrr], key="wkrr")
            for t in range(NT):
                b = t % 2
                ts_ = slice(t * 128, (t + 1) * 128)
                MM(bank(b), [(hT[:, kc, ts_], wlat[:, kc, :]) for kc in range(KC)], reads=[T_wlat, T_hT[t]])
                rr = []
                for j in range(2):
                    ss, t_ss = stat()
                    ACT(junk2, bank(b, j * 256, j * 256 + 256), AF.Square, writes=[t_ss, T_junk2], accum_out=ss)
                    rr.append(rstd_from_ss(ss, t_ss, 256))
                for j in range(2):
                    ACT(cl[b][:, j * 256:(j + 1) * 256], bank(b, j * 256, j * 256 + 256), AF.Copy, reads=[rr[j][1]],
                        writes=[T_cl[b]], scale=rr[j][0])
                pbf = bank_bf(6 + b)
                TR([(pbf[:, k * 128:(k + 1) * 128], cl[b][:, k * 128:(k + 1) * 128]) for k in range(4)], ident, reads=[T_cl[b]])
                CP("dve", cqT[:, :, ts_], r3(pbf[:, 0:256], 128), writes=[T_cq[t]])
                CP("dve", ckvT[:, :, ts_], r3(pbf[:, 256:512], 128), writes=[T_ckv[t]])

            def rope_a(pa, bs):
                TT("dve", t1[0:64, :], pa, cosT[0:64, bs], ALU.mult, reads=[T_cos], writes=[T_t1])

            def rope_b(pb_, dst, t_dst, bs):
                TT("dve", t2[0:64, :], pb_, sinT[0:64, bs], ALU.mult, reads=[T_sin], writes=[T_t2])
                TT("pool", dst, t1[0:64, :], t2[0:64, :], ALU.add, reads=[T_t1, T_t2], writes=[t_dst])

            for blk in range(4):
                bs = slice(blk * 512, (blk + 1) * 512)
                rd = T_hT[4 * blk:4 * blk + 4]
                MM(bank(2)[0:64, :], [(wkr[:, kc, :], hT[:, kc, bs]) for kc in range(KC)], reads=[T_wkr] + rd)
                rope_a(bank(2)[0:64, :], bs)
                MM(bank(3)[0:64, :], [(wkrr[:, kc, :], hT[:, kc, bs]) for kc in range(KC)], reads=[T_wkrr] + rd)
                rope_b(bank(3)[0:64, :], kpeT[0:64, bs], T_kpe[blk], bs)
            p.barrier()
            AR.off = mark_m
            hb_[1] = head_bufs()

            SCALE = 192.0 ** -0.5
            pbk = [0]

            def prep_bank():
                b = 6 + (pbk[0] % 2)
                pbk[0] += 1
                return b

            def load_head_w(h):
                w = hw_[h % 2]
                DMA(w["qn"], wslice(s_wuq, h * 192, 128), writes=[w["T"]["qn"]], key="wqn%d" % (h % 2))
                DMA(w["qp"], wslice(s_wuq, h * 192 + 128, 64), writes=[w["T"]["qp"]], key="wqp%d" % (h % 2))
                DMA(w["qr"], wslice(s_wuq, 1536 + h * 64, 64), writes=[w["T"]["qr"]], key="wqr%d" % (h % 2))
                DMA(w["kn"], wslice(s_wuk, h * 128, 128), writes=[w["T"]["kn"]], key="wkn%d" % (h % 2))
                DMA(w["vn"], wslice(s_wuv, h * 128, 128), writes=[w["T"]["vn"]], key="wvn%d" % (h % 2))

            def prep_groups(h):
                w = hw_[h % 2]
                hb = hb_[h % 2]
                gs = []
                for blk in range(4):
                    bs = slice(blk * 512, (blk + 1) * 512)
                    rq = T_cq[4 * blk:4 * blk + 4]
                    rk = T_ckv[4 * blk:4 * blk + 4]

                    def g_qn(bs=bs, rq=rq, blk=blk):
                        b = prep_bank()
                        MM(bank(b), [(w["qn"][:, kc, :], cqT[:, kc, bs]) for kc in range(2)], reads=[w["T"]["qn"]] + rq)
                        CP("act", hb["qn"][:, bs], bank(b), writes=[hb["T_qn"][blk]])

                    def g_kn(bs=bs, rk=rk, blk=blk):
                        b = prep_bank()
                        MM(bank(b), [(w["kn"][:, kc, :], ckvT[:, kc, bs]) for kc in range(2)], reads=[w["T"]["kn"]] + rk)
                        CP("act", hb["kn"][:, bs], bank(b), writes=[hb["T_kn"][blk]])

                    def g_qa(bs=bs, rq=rq, blk=blk):
                        b = prep_bank()
                        MM(bank(b)[0:64, :], [(w["qp"][:, kc, :], cqT[:, kc, bs]) for kc in range(2)], reads=[w["T"]["qp"]] + rq)
                        rope_a(bank(b)[0:64, :], bs)

                    def g_qb(bs=bs, rq=rq, blk=blk):
                        b = prep_bank()
                        MM(bank(b)[0:64, :], [(w["qr"][:, kc, :], cqT[:, kc, bs]) for kc in range(2)], reads=[w["T"]["qr"]] + rq)
                        rope_b(bank(b)[0:64, :], hb["qp"][0:64, bs], hb["T_qp"][blk], bs)

                    def g_v(blk=blk):
                        b = prep_bank()
                        pv = bank(b)
                        for tt in range(4):
                            t = blk * 4 + tt
                            MM(pv[:, tt * 128:(tt + 1) * 128], [(ckvT[:, kc, t * 128:(t + 1) * 128], w["vn"][:, kc, :]) for kc in range(2)],
                               reads=[w["T"]["vn"], T_ckv[t]])
                        CP("dve", hb["v"][:, blk * 4:(blk + 1) * 4, :], r3(pv, 128), writes=[hb["T_v"][blk]])
                    gs += [g_qn, g_kn, g_qa, g_qb, g_v]
                return gs

            load_head_w(0)
            for g in prep_groups(0):
                g()
            for h in range(8):
                hb = hb_[h % 2]
                pend = []
                if h + 1 < 8:
                    load_head_w(h + 1)
                    pend = prep_groups(h + 1)
                iters = [(qb, kt) for qb in range(4) for kt in range(NT)]

                def emit_S(i, hb=hb):
                    qb, kt = iters[i]
                    sl = i % 3
                    ps = i % NPT
                    qs = slice(qb * 512, (qb + 1) * 512)
                    ks = slice(kt * 128, (kt + 1) * 128)
                    MM(bank(sl), [(hb["kn"][:, ks], hb["qn"][:, qs]), (kpeT[:, ks], hb["qp"][:, qs])],
                       reads=[hb["T_kn"][kt // 4], hb["T_qn"][qb], T_kpe[kt // 4], hb["T_qp"][qb]])
                    ACT(PTs[ps], bank(sl), AF.Exp, writes=[T_PT[ps]], scale=SCALE)
                emit_S(0)
                emit_S(1)
                for i, (qb, kt) in enumerate(iters):
                    ps = i % NPT
                    ob = qb % 2

                    def pv_fn(e, kt=kt, ps=ps, ob=ob, hb=hb):
                        return e.matmul(bank(3 + ob), lhsT=hb["v"][:, kt, :], rhs=PTs[ps], start=(kt == 0), stop=(kt == NT - 1))
                    p.op("pe", pv_fn, reads=[T_PT[ps], hb["T_v"][kt // 4]], writes=[LOCK[3 + ob]])
                    ze = kt % 2
                    zeng = "dve" if ze == 0 else "pool"
                    if kt < 2:
                        CP(zeng, zacc[ze][ob], PTs[ps], reads=[T_PT[ps]], writes=[T_zacc[ze][ob]])
                    else:
                        TT(zeng, zacc[ze][ob], zacc[ze][ob], PTs[ps], ALU.add, reads=[T_PT[ps], T_zacc[ze][ob]], writes=[T_zacc[ze][ob]])
                    if i + 2 < len(iters):
                        emit_S(i + 2)
                    if pend and i % 3 == 2:
                        pend.pop(0)()
                    if kt == NT - 1:
                        qs = slice(qb * 512, (qb + 1) * 512)
                        MM(bank(5), [(ones_f, zacc[0][ob]), (ones_f, zacc[1][ob])], reads=[T_zacc[0][ob], T_zacc[1][ob]])
                        p.op("dve", lambda e: e.reciprocal(out=rz, in_=bank(5)), writes=[T_rz, LOCK[5]])
                        TT("dve", BT[:, h, qs], bank(3 + ob), rz, ALU.mult, reads=[T_rz], writes=[T_om[h][qb]])
                while pend:
                    pend.pop(0)()
            p.barrier()
            if dbg and s == 0:
                DMA(dbg_out["d_omT"], BT.rearrange("p a b -> p (a b)"), key="dbg")
                p.barrier()

            stage('M')
            gate_merge(OFF_GB, s_wom, False, "gb")
            stage('gb')
            if dbg and s == 0:
                DMA(dbg_out["d_C"], CT.rearrange("p a b -> p (a b)"), key="dbg")
                p.barrier()

            AR.off = X0
            FB = 4
            wout = r3(AR.bf(8 * D), D)
            T_wout = T()
            hT_f32 = hT.rearrange("p a b -> p (a b)").bitcast(F32)
            x1 = [r3(hT_f32[:, i * FB * D:(i + 1) * FB * D], D) for i in range(2)]
            T_x1 = [[T() for _ in range(FB)] for _ in range(2)]
            xr = [AR.f32(D) for _ in range(2)]
            T_xr = [T() for _ in range(2)]
            tmpn = [AR.f32(D) for _ in range(2)]
            T_tmp = [T() for _ in range(2)]
            tmpy, T_tmpy = tmpn, T_tmp
            xs2 = [AR.bf(D) for _ in range(FB)]
            T_xs2 = [T() for _ in range(FB)]
            h2T = r3(AR.bf(8 * 512), 512)
            T_h2 = [T() for _ in range(FB)]
            aT = r3(BT.rearrange("p a b -> p (a b)")[:, 0:NF * 512], 512)
            T_a = [T() for _ in range(NF)]
            NWS = 3
            wgs = [r3(AR.bf(8 * 128), 128) for _ in range(NWS)]
            wus = [r3(AR.bf(8 * 128), 128) for _ in range(NWS)]
            T_wgs = [T() for _ in range(NWS)]
            T_wus = [T() for _ in range(NWS)]
            bflat = BT.rearrange("p a b -> p (a b)")
            wds = [AR.bf(D) for _ in range(NF - 5)] + [bflat[:, NF * 512 + j * D:NF * 512 + (j + 1) * D] for j in range(5)]
            T_wds = [T() for _ in range(NF)]
            sgt = [AR.f32(512) for _ in range(2)]
            T_sgt = [T() for _ in range(2)]
            junk3 = AR.bf(512)
            T_junk3 = T()
            T_dummy = [T(), T()]
            DMA(wout, s_wout, writes=[T_wout], key="wout")
            rbk = [0]

            def nbank():
                b = rbk[0] % 8
                rbk[0] += 1
                return b

            def post_norm_parts(b0, b1):
                ssa, t_a = stat()
                ssb, t_b = stat()
                ACT(junk3, bank(b0), AF.Square, writes=[t_a, T_junk3], accum_out=ssa)
                ACT(junk3, bank(b1), AF.Square, writes=[t_b, T_junk3], accum_out=ssb)
                sst, t_s = stat()
                TT("dve", sst, ssa, ssb, ALU.add, reads=[t_a, t_b], writes=[t_s])
                return rstd_from_ss(sst, t_s, D)

            def p3_tiles(blk):
                xb = blk % 2
                for tt in range(FB):
                    t = blk * FB + tt
                    ts_ = slice(t * 128, (t + 1) * 128)
                    i2 = tt % 2
                    b0, b1 = nbank(), nbank()
                    rd = [T_C[m][blk] for m in range(8)] + [T_wout]
                    MM(bank(b0), [(CT[:, kc, ts_], wout[:, kc, 0:512]) for kc in range(KC)], reads=rd)
                    MM(bank(b1), [(CT[:, kc, ts_], wout[:, kc, 512:1024]) for kc in range(KC)], reads=rd)
                    DMA(xr[i2], x_d[tok0 + t * 128:tok0 + (t + 1) * 128, :], writes=[T_xr[i2]], key="xr%d" % i2)
                    r, t_r = post_norm_parts(b0, b1)
                    STT(tmpn[i2][:, 0:512], bank(b0), r, g_p1[:, 0:512], ALU.mult, ALU.mult, reads=[t_r], writes=[T_tmp[i2]])
                    STT(tmpn[i2][:, 512:1024], bank(b1), r, g_p1[:, 512:1024], ALU.mult, ALU.mult, reads=[t_r], writes=[T_tmp[i2]])
                    TT("pool", x1[xb][:, tt, :], tmpn[i2], xr[i2], ALU.add, reads=[T_tmp[i2], T_xr[i2]], writes=[T_x1[xb][tt]])
                    norm_part(x1[xb][:, tt, :], T_x1[xb][tt], xs2[tt], T_xs2[tt])

            def p3_trans(blk):
                for tt in range(FB):
                    tr_part(xs2[tt], T_xs2[tt], h2T[:, :, tt * 128:(tt + 1) * 128], T_h2[tt], nbank())

            def load_gu(f):
                DMA(wgs[f % NWS], wslice(s_wg, f * 128, 128), writes=[T_wgs[f % NWS]], key="wg%d" % (f % NWS))
                DMA(wus[f % NWS], wslice(s_wu, f * 128, 128), writes=[T_wus[f % NWS]], key="wu%d" % (f % NWS))

            for f in range(NF):
                DMA(wds[f], s_wd[:, f, :], writes=[T_wds[f]], key="wd%d" % f)

            def ffn1(blk):
                load_gu(0)
                load_gu(1)
                for f in range(NF):
                    if f + 2 < NF:
                        load_gu(f + 2)
                    i2 = f % 2
                    bg, bu = nbank(), nbank()
                    MM(bank(bg), [(wgs[f % NWS][:, kc, :], h2T[:, kc, :]) for kc in range(KC)], reads=[T_wgs[f % NWS]] + T_h2)
                    MM(bank(bu), [(wus[f % NWS][:, kc, :], h2T[:, kc, :]) for kc in range(KC)], reads=[T_wus[f % NWS]] + T_h2)
                    ACT(sgt[i2], bank(bg), AF.Silu, writes=[T_sgt[i2]])
                    TT("dve", aT[:, f, :], sgt[i2], bank(bu), ALU.mult, reads=[T_sgt[i2]], writes=[T_a[f]])

            def ffn2(blk):
                xb = blk % 2
                for pr in range(FB // 2):
                    bks = [nbank() for _ in range(4)]
                    for f in range(NF):
                        w = wds[f]

                        def dfn(e, f=f, w=w, pr=pr, bks=bks):
                            for j in range(2):
                                tt = pr * 2 + j
                                a = aT[:, f, tt * 128:(tt + 1) * 128]
                                e.matmul(bank(bks[2 * j]), lhsT=a, rhs=w[:, 0:512], start=(f == 0), stop=(f == NF - 1))
                                ins = e.matmul(bank(bks[2 * j + 1]), lhsT=a, rhs=w[:, 512:1024], start=(f == 0), stop=(f == NF - 1))
                            return ins
                        p.op("pe", dfn, reads=[T_a[f], T_wds[f]], writes=[LOCK[b] for b in bks])
                    for j in range(2):
                        tt = pr * 2 + j
                        t = blk * FB + tt
                        b0, b1 = bks[2 * j], bks[2 * j + 1]
                        r, t_r = post_norm_parts(b0, b1)
                        STT(tmpy[j][:, 0:512], bank(b0), r, g_p2[:, 0:512], ALU.mult, ALU.mult, reads=[t_r], writes=[T_tmpy[j]])
                        STT(tmpy[j][:, 512:1024], bank(b1), r, g_p2[:, 512:1024], ALU.mult, ALU.mult, reads=[t_r], writes=[T_tmpy[j]])
                        TT("pool", tmpy[j], tmpy[j], x1[xb][:, tt, :], ALU.add, reads=[T_tmpy[j], T_x1[xb][tt]], writes=[T_tmpy[j]])
                        DMA(y_d[tok0 + t * 128:tok0 + (t + 1) * 128, :], tmpy[j], reads=[T_tmpy[j]], key="y%d" % j, eng="pool")

            p3_tiles(0)
            p3_trans(0)
            for blk in range(4):
                ffn1(blk)
                if blk + 1 < 4:
                    p3_tiles(blk + 1)
                ffn2(blk)
                if blk + 1 < 4:
                    p3_trans(blk + 1)
            p.barrier()


    try:
        _build_body()
    except _Stop:
        p.barrier()
    p.emit()
    return nc, list(dbg_out.keys())


def _host_consts():
    c = {}
    c["ident"] = np.eye(128, dtype=np.float32).astype(ml_dtypes.bfloat16)
    inv = (10000.0 ** (-np.arange(0, 64, 2, dtype=np.float32) / np.float32(64))).astype(np.float32)
    ang = (np.arange(L, dtype=np.float32)[:, None] * inv[None, :]).astype(np.float32)
    cos = np.cos(ang).astype(np.float32).T
    sin = np.sin(ang).astype(np.float32).T
    c["cosT"] = np.ascontiguousarray(np.concatenate([cos, cos], axis=0))
    c["sinT"] = np.ascontiguousarray(np.concatenate([sin, sin], axis=0))
    j = np.arange(128)[:, None]
    i = np.arange(128)[None, :]
    c["mask_f"] = (i >= j).astype(np.float32)
    c["mask_b"] = (i <= j).astype(np.float32)
    return c


def _pk(v, n):
    return np.ascontiguousarray(np.asarray(v, np.float32).reshape(n, 128).T)


def make_in_maps(inputs, x_shards):
    f = lambda k: np.ascontiguousarray(np.asarray(inputs[k], np.float32)[0])
    common = dict(
        w_in=f("w_in"), w_o_gla=f("w_o_gla"), w_uq=f("w_uq"), w_uk=f("w_uk"), w_uv=f("w_uv"),
        w_o_mla=f("w_o_mla"), w_out=f("w_out"), w_gate=f("w_gate"), w_up=f("w_up"), w_down=f("w_down"),
        g_pre=_pk(f("norm_mix_pre"), 8), g_q=_pk(f("mla_norm_q"), 2), g_kv=_pk(f("mla_norm_kv"), 2),
        g_gla=_pk(f("gla_norm"), 2), g_ffn=_pk(f("norm_ffn_pre"), 8),
        g_post1=np.ascontiguousarray(np.broadcast_to(f("norm_mix_post")[None, :], (128, D))),
        g_post2=np.ascontiguousarray(np.broadcast_to(f("norm_ffn_post")[None, :], (128, D))),
        wa_f=f("gla_wa_fwd"), wa_b=f("gla_wa_bwd"),
        ba_f=_pk(f("gla_ba_fwd"), 4), ba_b=_pk(f("gla_ba_bwd"), 4),
    )
    common.update(_host_consts())
    return [dict(common, x=xs) for xs in x_shards]


def kernel(**inputs):
    xp = np.asarray(inputs["x_prompt"], np.float32)
    xs = np.asarray(inputs["x_sample"], np.float32)
    nb_p, nb_s = xp.shape[0], xs.shape[0]
    allx = np.concatenate([xp.reshape(nb_p, L, D), xs.reshape(nb_s, L, D)], axis=0)
    nseq_total = allx.shape[0]
    per = nseq_total // N_CORES
    shards = [np.ascontiguousarray(allx[c * per:(c + 1) * per].reshape(per * L, D)) for c in range(N_CORES)]
    nc, _ = build_program(per)
    in_maps = make_in_maps(inputs, shards)
    res = run_bass_kernel_spmd(nc, in_maps, core_ids=list(range(N_CORES)))
    y = np.concatenate([np.asarray(r["y"], np.float32).reshape(per, L, D) for r in res.results], axis=0)
    return (np.ascontiguousarray(y[:nb_p]), np.ascontiguousarray(y[nb_p:]))
```
